# Optimizing a Trainium2 kernel written in Bass

```python
import math
import jax, jax.numpy as jnp
from jax import lax
import numpy as np

D_MODEL = 1024
BATCH = 16
SEQ = 256
DEPTH = 2
DEC_BATCH = 4
DEC_SEQ = 2048
PAST_LEN = 256

GRID_W = 64
N_DIR = 2
S5_WIDTH = D_MODEL // 2
S5_GROUP_CH = 16
S5_GROUPS = S5_WIDTH // S5_GROUP_CH
S5_STATE = 64
GLA_WIDTH = D_MODEL // 2
GLA_HEADS = 4
GLA_DV = GLA_WIDTH // GLA_HEADS
GLA_DK = GLA_DV // 2
GLA_KEY_WIDTH = GLA_HEADS * GLA_DK
GLA_GATE_RANK = 16
GLA_GATE_NORM = 16.0
GLA_CHUNK = 32
EPS = 1e-6
IN_WIDTHS = (S5_WIDTH, S5_WIDTH, GLA_KEY_WIDTH, GLA_KEY_WIDTH, GLA_WIDTH, GLA_WIDTH, GLA_GATE_RANK, D_MODEL, D_MODEL)
D_IN = sum(IN_WIDTHS)

kernel_name = 'hybrid_s5_gla_flow_step'


def rmsnorm(x, g):
    xf = x.astype(jnp.float32)
    y = xf * lax.rsqrt(jnp.mean(xf * xf, axis=-1, keepdims=True) + EPS)
    return (y * g.astype(jnp.float32)).astype(x.dtype)


def _split_points():
    pts, acc = [], 0
    for w in IN_WIDTHS[:-1]:
        acc += w
        pts.append(acc)
    return pts


def grid_pos_embed(length, dim, dtype):
    rows = length // GRID_W
    quarter = dim // 4
    freqs = jnp.exp(-math.log(10000.0) * jnp.arange(quarter, dtype=jnp.float32) / quarter)

    def sincos(pos):
        ang = pos.astype(jnp.float32)[:, None] * freqs[None, :]
        return jnp.concatenate([jnp.sin(ang), jnp.cos(ang)], axis=-1)

    er = sincos(jnp.arange(rows))
    ec = sincos(jnp.arange(GRID_W))
    pe = jnp.concatenate([jnp.broadcast_to(er[:, None, :], (rows, GRID_W, dim // 2)),
                          jnp.broadcast_to(ec[None, :, :], (rows, GRID_W, dim // 2))], axis=-1)
    return pe.reshape(rows * GRID_W, dim).astype(dtype)


def _lin_combine(e1, e2):
    a1, b1 = e1
    a2, b2 = e2
    return a2 * a1, a2 * b1 + b2


def s5_direction(u, lam_re, lam_im, log_dt, b_c, h0):
    lam = lax.complex(lam_re.astype(jnp.float32), lam_im.astype(jnp.float32))
    dt = jnp.exp(log_dt.astype(jnp.float32))[:, None]
    a_bar = jnp.exp(lam * dt)
    b_bar = ((a_bar - 1.0) / lam)[..., None] * b_c
    bu = jnp.einsum('gpc,blgc->blgp', b_bar, u.astype(jnp.complex64))
    bu = bu.at[:, 0].add(a_bar[None] * h0)
    a = jnp.broadcast_to(a_bar, bu.shape)
    _, h = lax.associative_scan(_lin_combine, (a, bu), axis=1)
    return h


def s5_branch(u, p, h0_re, h0_im):
    bsz, length, _ = u.shape
    uf = u.astype(jnp.float32).reshape(bsz, length, S5_GROUPS, S5_GROUP_CH)
    b_c = lax.complex(p['s5_b_re'].astype(jnp.float32), p['s5_b_im'].astype(jnp.float32))
    c_c = lax.complex(p['s5_c_re'].astype(jnp.float32), p['s5_c_im'].astype(jnp.float32))
    h0 = lax.complex(h0_re.astype(jnp.float32), h0_im.astype(jnp.float32))
    h_f = s5_direction(uf, p['s5_lam_re'][0], p['s5_lam_im'][0], p['s5_log_dt'][0], b_c, h0[:, 0])
    h_b = jnp.flip(s5_direction(jnp.flip(uf, 1), p['s5_lam_re'][1], p['s5_lam_im'][1],
                                p['s5_log_dt'][1], b_c, h0[:, 1]), 1)
    y = jnp.einsum('gcp,blgp->blgc', c_c, h_f + h_b).real.reshape(bsz, length, S5_WIDTH)
    y = y + p['s5_d'].astype(jnp.float32) * uf.reshape(bsz, length, S5_WIDTH)
    y = jax.nn.gelu(y).astype(u.dtype)
    y = y * jax.nn.sigmoid(y @ p['w_glu'] + p['b_glu'])
    h_final = jnp.stack([h_f[:, -1], h_b[:, 0]], axis=1)
    return y, h_final.real, h_final.imag


def gla_chunked(q, k, v, g, s0):
    bsz, length, nh, dk = q.shape
    dv = v.shape[-1]
    n = length // GLA_CHUNK
    q = q.astype(jnp.float32).reshape(bsz, n, GLA_CHUNK, nh, dk)
    k = k.astype(jnp.float32).reshape(bsz, n, GLA_CHUNK, nh, dk)
    v = v.astype(jnp.float32).reshape(bsz, n, GLA_CHUNK, nh, dv)
    g = g.astype(jnp.float32).reshape(bsz, n, GLA_CHUNK, nh, dk)
    b = jnp.cumsum(g, axis=2)
    b_last = b[:, :, -1]
    causal = jnp.tril(jnp.ones((GLA_CHUNK, GLA_CHUNK), dtype=bool))[None, None, :, :, None, None]
    diff = b[:, :, :, None] - b[:, :, None, :]
    decay = jnp.exp(jnp.where(causal, diff, -jnp.inf))
    scores = jnp.einsum('bnihd,bnijhd,bnjhd->bnhij', q, decay, k)
    o_intra = jnp.einsum('bnhij,bnjhe->bnihe', scores, v)
    k_to_end = k * jnp.exp(b_last[:, :, None] - b)
    u_chunk = jnp.einsum('bnjhd,bnjhe->bnhde', k_to_end, v)
    a_chunk = jnp.exp(b_last)

    def step(s, inp):
        a_c, u_c = inp
        return a_c[..., None] * s + u_c, s

    s_final, s_start = lax.scan(step, s0.astype(jnp.float32),
                                (jnp.moveaxis(a_chunk, 1, 0), jnp.moveaxis(u_chunk, 1, 0)))
    s_start = jnp.moveaxis(s_start, 0, 1)
    o_inter = jnp.einsum('bnihd,bnhde->bnihe', q * jnp.exp(b), s_start)
    o = (o_intra + o_inter).reshape(bsz, length, nh, dv)
    return o, s_final


def gla_branch(q_in, k_in, v_in, g_low, p, s0):
    bsz, length, _ = q_in.shape
    q = q_in.reshape(bsz, length, GLA_HEADS, GLA_DK) * (GLA_DK ** -0.5)
    k = k_in.reshape(bsz, length, GLA_HEADS, GLA_DK)
    v = v_in.reshape(bsz, length, GLA_HEADS, GLA_DV)

    def log_decay(d):
        logits = (g_low @ p['gla_wg_up'][d] + p['gla_bg'][d]).astype(jnp.float32)
        return (jax.nn.log_sigmoid(logits) / GLA_GATE_NORM).reshape(bsz, length, GLA_HEADS, GLA_DK)

    o_f, s_f = gla_chunked(q, k, v, log_decay(0), s0[:, 0])
    o_b, s_b = gla_chunked(jnp.flip(q, 1), jnp.flip(k, 1), jnp.flip(v, 1), jnp.flip(log_decay(1), 1), s0[:, 1])
    o = o_f + jnp.flip(o_b, 1)
    o = rmsnorm(o, p['gla_norm_g'].reshape(GLA_HEADS, GLA_DV)).reshape(bsz, length, GLA_WIDTH)
    return o.astype(q_in.dtype), jnp.stack([s_f, s_b], axis=1)


def trunk_layer(x, cond, p, s5_re0, s5_im0, gla0):
    mod = jax.nn.silu(cond) @ p['w_mod'] + p['b_mod']
    shift, scale, gate = jnp.split(mod[:, None, :], 3, axis=-1)
    h = rmsnorm(x, p['norm_g']) * (1.0 + scale) + shift
    proj = h @ p['w_in']
    u_a, gate_a, q, k, v, gate_b, g_low, m_a, m_b = jnp.split(proj, _split_points(), axis=-1)
    y_a, s5_re, s5_im = s5_branch(u_a, p, s5_re0, s5_im0)
    y_a = y_a * jax.nn.silu(gate_a)
    y_b, gla_s = gla_branch(q, k, v, g_low, p, gla0)
    y_b = y_b * jax.nn.silu(gate_b)
    merged = jax.nn.sigmoid(m_a) * (y_a @ p['w_pa']) + jax.nn.sigmoid(m_b) * (y_b @ p['w_pb'])
    x = x + gate * (merged @ p['w_o'])
    return x, s5_re, s5_im, gla_s


def setup_inputs(seed: int = 0) -> dict:
    key = jax.random.key(seed)
    ks = jax.random.split(key, 32)
    nrm = lambda i, shape, s=1.0: s * jax.random.normal(ks[i], shape, jnp.float32)
    d = D_MODEL
    lam_im_base = math.pi * jnp.arange(S5_STATE, dtype=jnp.float32)
    return {
        'x_prompt': nrm(0, (BATCH, SEQ, d)),
        'x_sample': nrm(1, (DEC_BATCH, DEC_SEQ, d)),
        'c': nrm(2, (DEC_BATCH, d)),
        'state_s5_re': nrm(3, (DEC_BATCH, DEPTH, N_DIR, S5_GROUPS, S5_STATE), 0.3),
        'state_s5_im': nrm(4, (DEC_BATCH, DEPTH, N_DIR, S5_GROUPS, S5_STATE), 0.3),
        'state_gla': nrm(5, (DEC_BATCH, DEPTH, N_DIR, GLA_HEADS, GLA_DK, GLA_DV), 0.3),
        'c_ctx': nrm(6, (d,)),
        'norm_g': 1.0 + nrm(7, (DEPTH, d), 0.02),
        'w_mod': nrm(8, (DEPTH, d, 3 * d), d ** -0.5),
        'b_mod': nrm(9, (DEPTH, 3 * d), 0.02),
        'w_in': nrm(10, (DEPTH, d, D_IN), d ** -0.5),
        'gla_wg_up': nrm(11, (DEPTH, N_DIR, GLA_GATE_RANK, GLA_KEY_WIDTH), GLA_GATE_RANK ** -0.5),
        'gla_bg': nrm(12, (DEPTH, N_DIR, GLA_KEY_WIDTH), 0.1),
        'gla_norm_g': 1.0 + nrm(13, (DEPTH, GLA_WIDTH), 0.02),
        's5_lam_re': -0.5 + nrm(14, (DEPTH, N_DIR, S5_GROUPS, S5_STATE), 0.01),
        's5_lam_im': lam_im_base + nrm(15, (DEPTH, N_DIR, S5_GROUPS, S5_STATE), 0.01),
        's5_log_dt': jax.random.uniform(ks[16], (DEPTH, N_DIR, S5_GROUPS), jnp.float32,
                                        math.log(1e-3), math.log(1e-1)),
        's5_b_re': nrm(17, (DEPTH, S5_GROUPS, S5_STATE, S5_GROUP_CH), (2 * S5_GROUP_CH) ** -0.5),
        's5_b_im': nrm(18, (DEPTH, S5_GROUPS, S5_STATE, S5_GROUP_CH), (2 * S5_GROUP_CH) ** -0.5),
        's5_c_re': nrm(19, (DEPTH, S5_GROUPS, S5_GROUP_CH, S5_STATE), S5_STATE ** -0.5),
        's5_c_im': nrm(20, (DEPTH, S5_GROUPS, S5_GROUP_CH, S5_STATE), S5_STATE ** -0.5),
        's5_d': nrm(21, (DEPTH, S5_WIDTH)),
        'w_glu': nrm(22, (DEPTH, S5_WIDTH, S5_WIDTH), S5_WIDTH ** -0.5),
        'b_glu': nrm(23, (DEPTH, S5_WIDTH), 0.02),
        'w_pa': nrm(24, (DEPTH, S5_WIDTH, d), S5_WIDTH ** -0.5),
        'w_pb': nrm(25, (DEPTH, GLA_WIDTH, d), GLA_WIDTH ** -0.5),
        'w_o': nrm(26, (DEPTH, d, d), d ** -0.5),
        'final_norm_g': 1.0 + nrm(27, (d,), 0.02),
    }


def reference(x_prompt, x_sample, c, state_s5_re, state_s5_im, state_gla, c_ctx, norm_g, w_mod, b_mod,
              w_in, gla_wg_up, gla_bg, gla_norm_g, s5_lam_re, s5_lam_im, s5_log_dt, s5_b_re, s5_b_im,
              s5_c_re, s5_c_im, s5_d, w_glu, b_glu, w_pa, w_pb, w_o, final_norm_g):
    bp = x_prompt.shape[0]
    xp = x_prompt
    xs = x_sample + grid_pos_embed(x_sample.shape[1], D_MODEL, x_sample.dtype)[None]
    zero_s5 = jnp.zeros((bp, N_DIR, S5_GROUPS, S5_STATE), jnp.float32)
    zero_gla = jnp.zeros((bp, N_DIR, GLA_HEADS, GLA_DK, GLA_DV), jnp.float32)
    cond_ctx = c_ctx[None]
    new_re, new_im, new_gla = [], [], []
    for l in range(DEPTH):
        p = {'norm_g': norm_g[l], 'w_mod': w_mod[l], 'b_mod': b_mod[l], 'w_in': w_in[l],
             'gla_wg_up': gla_wg_up[l], 'gla_bg': gla_bg[l], 'gla_norm_g': gla_norm_g[l],
             's5_lam_re': s5_lam_re[l], 's5_lam_im': s5_lam_im[l], 's5_log_dt': s5_log_dt[l],
             's5_b_re': s5_b_re[l], 's5_b_im': s5_b_im[l], 's5_c_re': s5_c_re[l], 's5_c_im': s5_c_im[l],
             's5_d': s5_d[l], 'w_glu': w_glu[l], 'b_glu': b_glu[l], 'w_pa': w_pa[l], 'w_pb': w_pb[l],
             'w_o': w_o[l]}
        xp, re_l, im_l, gla_l = trunk_layer(xp, cond_ctx, p, zero_s5, zero_s5, zero_gla)
        new_re.append(re_l)
        new_im.append(im_l)
        new_gla.append(gla_l)
        xs, _, _, _ = trunk_layer(xs, c, p, state_s5_re[:, l], state_s5_im[:, l], state_gla[:, l])
    y_prompt = rmsnorm(xp, final_norm_g)
    y_sample = rmsnorm(xs, final_norm_g)
    new_s5_re = jnp.stack(new_re, axis=1).astype(x_prompt.dtype)
    new_s5_im = jnp.stack(new_im, axis=1).astype(x_prompt.dtype)
    new_gla_state = jnp.stack(new_gla, axis=1).astype(x_prompt.dtype)
    return (y_prompt, y_sample, new_s5_re, new_s5_im, new_gla_state)
```

```python
import math
import numpy as np
from contextlib import ExitStack
import concourse.bass as bass
import concourse.mybir as mybir
from concourse.bass_utils import run_bass_kernel_spmd
from concourse.ap import AP

F32 = mybir.dt.float32
BF16 = mybir.dt.bfloat16
I32 = mybir.dt.int32
AF = mybir.ActivationFunctionType
ALU = mybir.AluOpType

D = 1024
DIN = 4624
DEPTH = 2
MAGIC = 12582912.0
TWO_PI = 2.0 * math.pi


class MK:
    def __init__(self, nc, es, needed=None):
        import bisect
        self._bisect = bisect
        self.needed = needed
        self.used = set()
        if needed is not None:
            self.need_sorted = {e: sorted(n for (ee, n) in needed if ee == e) for e in ["pe", "act", "dve", "pool"]}
        self.nc = nc
        self.eng = {"pe": nc.tensor, "act": nc.scalar, "dve": nc.vector, "pool": nc.gpsimd, "sp": nc.sync}
        self.sem = {}
        self.cnt = {}
        self.waited = {e: {} for e in self.eng}
        for e in ["pe", "act", "dve", "pool"]:
            self.sem[e] = es.enter_context(nc.semaphore("s_" + e))
            self.cnt[e] = 0
        self.dma_sems = []
        for i in range(8):
            self.dma_sems.append([es.enter_context(nc.semaphore("s_dma%d" % i)), 0])
        self.dma_rr = 0
        self.bufs = {}
        self.nins = 0
        self.bar_sem = es.enter_context(nc.semaphore("s_bar"))
        self.bar_cnt = 0

    def _wait(self, e, tok):
        sem, val, key = tok[0], tok[1], tok[2]
        if self.waited[e].get(key, 0) >= val:
            return False
        sval = val
        if key in ("pe", "act", "dve", "pool"):
            self.used.add((key, val))
            if self.needed is not None:
                lst = self.need_sorted[key]
                sval = self._bisect.bisect_right(lst, val)
                assert sval > 0 and lst[sval - 1] == val, "wait on a non-signalling instruction"
        self.eng[e].wait_ge(sem, sval)
        self.waited[e][key] = val
        return True

    def _deps(self, reads, writes):
        deps = []
        for b in reads:
            st = self.bufs.setdefault(b, {"w": None, "r": {}})
            if st["w"] is not None:
                deps.append(st["w"])
        for b in writes:
            st = self.bufs.setdefault(b, {"w": None, "r": {}})
            if st["w"] is not None:
                deps.append(st["w"])
            deps.extend(st["r"].values())
        return deps

    def _mark(self, tok, reads, writes):
        for b in reads:
            self.bufs[b]["r"][tok[2]] = tok
        for b in writes:
            self.bufs[b]["w"] = tok
            self.bufs[b]["r"] = {}

    def op(self, e, fn, reads=(), writes=()):
        for tok in self._deps(reads, writes):
            if tok[3] == "pe" and e == "pe":
                continue
            self._wait(e, tok)
        ins = fn(self.eng[e])
        self.cnt[e] += 1
        self.nins += 1
        if self.needed is None or (e, self.cnt[e]) in self.needed:
            ins.then_inc(self.sem[e], 1)
        tok = (self.sem[e], self.cnt[e], e, e)
        self._mark(tok, reads, writes)
        return ins

    def dma(self, out, in_, reads=(), writes=(), q="sp"):
        slot = self.dma_sems[self.dma_rr]
        key = "dma%d" % self.dma_rr
        self.dma_rr = (self.dma_rr + 1) % len(self.dma_sems)
        if slot[1] > 0:
            self._wait(q, (slot[0], slot[1], key))
        for tok in self._deps(reads, writes):
            self._wait(q, tok)
        ins = self.eng[q].dma_start(out=out, in_=in_)
        slot[1] += 16
        self.nins += 1
        ins.then_inc(slot[0], 16)
        tok = (slot[0], slot[1], key, "dma")
        self._mark(tok, reads, writes)
        return tok

    def barrier(self):
        toks = [(self.sem[e], self.cnt[e], e) for e in ["pe", "act", "dve", "pool"] if self.cnt[e]]
        for i, slot in enumerate(self.dma_sems):
            if slot[1]:
                toks.append((slot[0], slot[1], "dma%d" % i))
        n = 0
        for t in toks:
            if self._wait("sp", t):
                n += 1
                if n % 6 == 0:
                    self.eng["sp"].nop()
        self.bar_cnt += 1
        self.eng["sp"].sem_inc(self.bar_sem, 1)
        for e in ["pe", "act", "dve", "pool"]:
            self._wait(e, (self.bar_sem, self.bar_cnt, "bar"))
            for t in toks:
                if self.waited[e].get(t[2], 0) < t[1]:
                    self.waited[e][t[2]] = t[1]
        self.bufs = {}

    def finish(self):
        n = 0
        toks = []
        for i, slot in enumerate(self.dma_sems):
            if slot[1]:
                toks.append((slot[0], slot[1], "dma%d" % i))
        for e in ["pe", "act", "dve", "pool"]:
            if self.cnt[e]:
                toks.append((self.sem[e], self.cnt[e], e))
        for t in toks:
            if self._wait("sp", t):
                n += 1
                if n % 6 == 0:
                    self.eng["sp"].nop()


PHASES = []


class _Stop(Exception):
    pass


def build(NSEG, dbg=(), stop=None, needed=None):
    T = NSEG * 256
    NT = T // 128
    NTT = T // 512
    NR = T // 8
    RT = NR // 128
    NROW = T // 64
    nc = bass.Bass("TRN2", target_bir_lowering=False)
    es = ExitStack()
    mk = MK(nc, es, needed)

    def din(name, shape):
        return nc.dram_tensor(name, list(shape), F32, kind="ExternalInput")

    def dout(name, shape):
        return nc.dram_tensor(name, list(shape), F32, kind="ExternalOutput")

    x_d = din("x", [T, D])
    cond_d = din("cond", [128, 8])
    flags_d = din("flags", [128, 2])
    s5ext_d = din("s5ext", [DEPTH, 128, NSEG * 64])
    glaext_d = din("glaext", [DEPTH, 2, 2, 128, 128])
    norm_g_d = din("norm_g", [DEPTH, D])
    w_mod_d = din("w_mod", [DEPTH, D, 3 * D])
    b_mod_d = din("b_mod", [DEPTH, 3 * D])
    w_in_d = din("w_in", [DEPTH, D, DIN])
    wgu_d = din("gla_wg_up", [DEPTH, 2, 16, 256])
    bg_d = din("gla_bg", [DEPTH, 2, 256])
    gng_d = din("gla_norm_g", [DEPTH, 512])
    lre_d = din("s5_lam_re", [DEPTH, 2, 32, 64])
    lim_d = din("s5_lam_im", [DEPTH, 2, 32, 64])
    ldt_d = din("s5_log_dt", [DEPTH, 2, 32])
    bre_d = din("s5_b_re", [DEPTH, 32, 64, 16])
    bim_d = din("s5_b_im", [DEPTH, 32, 64, 16])
    cre_d = din("s5_c_re", [DEPTH, 32, 16, 64])
    cim_d = din("s5_c_im", [DEPTH, 32, 16, 64])
    s5d_d = din("s5_d", [DEPTH, 512])
    w_glu_d = din("w_glu", [DEPTH, 512, 512])
    b_glu_d = din("b_glu", [DEPTH, 512])
    w_pa_d = din("w_pa", [DEPTH, 512, D])
    w_pb_d = din("w_pb", [DEPTH, 512, D])
    w_o_d = din("w_o", [DEPTH, D, D])
    fng_d = din("final_norm_g", [D])
    y_d = dout("y", [T, D])
    s5o_d = dout("s5o", [DEPTH, 128, NSEG * 64])
    glao_d = dout("glao", [DEPTH, NSEG, 2, 2, 128, 128])
    xs_d = nc.dram_tensor("xs_scr", [128, 8 * T], F32, kind="Internal")
    dbg_d = {k: dout("dbg_" + k, shp) for k, shp in dbg}

    uniq = [0]

    def sb(name, shape, dt, stack=es):
        uniq[0] += 1
        return stack.enter_context(nc.sbuf_tensor("%s_%d" % (name, uniq[0]), list(shape), dt))

    def V(t, rowlen, off, dims, p0=0, np_=128):
        return AP(t, p0 * rowlen + off, [[rowlen, np_]] + [list(d) for d in dims])

    def ncdma():
        return nc.allow_non_contiguous_dma(reason="small param layout loads")

    hT = sb("hT", [128, 8 * T], BF16)
    WST = [sb("wst%d" % i, [128, 2048], F32) for i in range(2)]
    WBF = [sb("wbf%d" % i, [128, 2048], BF16) for i in range(3)]
    ident_f = sb("ident_f", [128, 128], F32)
    ident_b = sb("ident_b", [128, 128], BF16)
    ones_f = sb("ones_f", [128, 128], F32)
    onesD = sb("onesD", [128, 128], BF16)
    onesH = sb("onesH", [128, 128], BF16)
    maskTz = sb("maskTz", [128, 2 * 128], F32)
    mask4 = sb("mask4", [128, 4 * 128], F32)
    maskc = sb("maskc", [128, T], BF16)
    flags = sb("flags_sb", [128, 2], F32)
    condT = sb("condT", [128, 8], F32)
    scond = sb("scond", [128, 8], BF16)
    modv = sb("modv", [128, 24], F32)
    gs = sb("gs", [128, 8], F32)
    ng = sb("ng", [128, 8], F32)
    bmod = sb("bmod", [128, 24], F32)
    fng = sb("fng", [128, 8], F32)
    zero8 = sb("zero8", [128, 8], F32)
    kk = sb("kk", [128, 17], F32)
    NB = {}
    PS = [es.enter_context(nc.psum_tensor("ps%d" % i, [128, 512], F32)) for i in range(8)]
    PSB = [p.bitcast(BF16) for p in PS]
    ps_rr = [0]

    def getps():
        i = ps_rr[0]
        ps_rr[0] = (i + 1) % 8
        return PS[i], PSB[i], "ps%d" % i

    def memset(e, t_ap, val, key):
        mk.op(e, lambda en: en.memset(t_ap, val), writes=[key])

    memset("pool", ones_f[:], 1.0, "ones_f")
    memset("pool", onesD[:], 1.0 / 1024.0, "onesD")
    memset("pool", onesH[:], 1.0 / 128.0, "onesH")
    memset("pool", zero8[:], 0.0, "zero8")
    mk.op("pool", lambda e: e.affine_select(ident_f[:], ones_f[:], [[-1, 128]], ALU.is_equal, 0.0, base=0, channel_multiplier=1),
          reads=["ones_f"], writes=["ident_f"])
    mk.op("pool", lambda e: e.tensor_copy(ident_b[:], ident_f[:]), reads=["ident_f"], writes=["ident_b"])
    mk.op("pool", lambda e: e.affine_select(V(maskTz, 256, 0, [[16, 8], [1, 16]]), V(ones_f, 128, 0, [[16, 8], [1, 16]]),
                                            [[16, 8], [0, 16]], ALU.is_ge, 0.0, base=15, channel_multiplier=-1),
          reads=["ones_f"], writes=["maskTz"])
    mk.op("pool", lambda e: e.affine_select(V(maskTz, 256, 128, [[16, 8], [1, 16]]), V(ones_f, 128, 0, [[16, 8], [1, 16]]),
                                            [[-16, 8], [0, 16]], ALU.is_ge, 0.0, base=0, channel_multiplier=1),
          reads=["ones_f"], writes=["maskTz"])
    for q in range(4):
        if q % 2 == 0:
            mk.op("pool", lambda e, q=q: e.affine_select(mask4[:, q * 128:(q + 1) * 128], ones_f[:], [[1, 128]], ALU.is_ge, 0.0,
                                                        base=0, channel_multiplier=-1), reads=["ones_f"], writes=["mask4"])
        else:
            mk.op("pool", lambda e, q=q: e.affine_select(mask4[:, q * 128:(q + 1) * 128], ones_f[:], [[-1, 128]], ALU.is_ge, 0.0,
                                                        base=0, channel_multiplier=1), reads=["ones_f"], writes=["mask4"])
    for n in range(NT):
        mk.op("pool", lambda e, n=n: e.affine_select(maskc[:, n * 128:(n + 1) * 128], ones_f[:], [[1, 128]], ALU.is_gt, 0.0,
                                                    base=0, channel_multiplier=0), reads=["ones_f"], writes=["maskc"])
    kki = sb("kki", [128, 32], I32)
    mk.op("pool", lambda e: e.iota(kki[:, 0:17], [[1, 17]], base=-8, channel_multiplier=0), writes=["kki"])
    mk.op("pool", lambda e: e.tensor_copy(kk[:], kki[:, 0:17]), reads=["kki"], writes=["kk"])
    vtmp = sb("vtmp", [32, 128], F32)
    sel2 = sb("sel2", [32, 32], F32)
    repm = sb("repm", [16, 128], F32)
    for g2 in range(2):
        mk.op("pool", lambda e, g2=g2: e.affine_select(sel2[:, g2 * 16:(g2 + 1) * 16], ones_f[0:32, 0:16], [[-2, 16]], ALU.is_equal, 0.0,
                                                      base=-g2, channel_multiplier=1), reads=["ones_f"], writes=["sel2"])
    mk.op("pool", lambda e: e.affine_select(V(repm, 128, 0, [[16, 8], [1, 16]], np_=16), V(ones_f, 128, 0, [[16, 8], [1, 16]], np_=16), [[0, 8], [1, 16]], ALU.is_equal, 0.0,
                                            base=0, channel_multiplier=-1), reads=["ones_f"], writes=["repm"])

    def load_vec(dst, dkey, rows_ap, n):
        mk.dma(vtmp[0:n, :], rows_ap, writes=["vtmp"])
        ps, _, pk = getps()
        mk.op("pe", lambda e: e.transpose(ps[:, 0:n], vtmp[0:n, :], ident_f[0:n, 0:n]), reads=["vtmp", "ident_f"], writes=[pk])
        mk.op("dve", lambda e: e.tensor_copy(dst, ps[:, 0:n]), reads=[pk], writes=[dkey])

    mk.dma(flags[:], flags_d.ap(), writes=["flags"])
    mk.dma(condT[:], cond_d.ap(), writes=["condT"])
    load_vec(fng[:], "fng", fng_d.ap().rearrange("(c p) -> c p", p=128), 8)
    mk.op("act", lambda e: e.activation(scond[:], condT[:], AF.Silu), reads=["condT"], writes=["scond"])

    wrr = [0, 0]

    def load_w(w2d, KT, c0, ncols, cast="act"):
        i = wrr[0]; wrr[0] = (i + 1) % 2
        j = wrr[1]; wrr[1] = (j + 1) % 3
        stv = WST[i][:, 0:KT * ncols].rearrange("p (k n) -> p k n", k=KT)
        bfv = WBF[j][:, 0:KT * ncols].rearrange("p (k n) -> p k n", k=KT)
        src = w2d[:, c0:c0 + ncols].rearrange("(k p) n -> p k n", p=128)
        with ncdma():
            mk.dma(stv, src, writes=["wst%d" % i])
        if cast == "act":
            mk.op("act", lambda e: e.copy(bfv, stv), reads=["wst%d" % i], writes=["wbf%d" % j])
        else:
            mk.op(cast, lambda e: e.tensor_copy(bfv, stv), reads=["wst%d" % i], writes=["wbf%d" % j])
        return bfv, "wbf%d" % j

    def hT_ap(kt, t0, n):
        return hT[:, kt * T + t0: kt * T + t0 + n]

    def sin_rr(out_ap, ang_ap, t1_ap, t2_ap, kout, kang, kt1, kt2):
        mk.op("dve", lambda e: e.tensor_scalar(t1_ap, ang_ap, 1.0 / TWO_PI, MAGIC, ALU.mult, ALU.add), reads=[kang], writes=[kt1])
        mk.op("dve", lambda e: e.tensor_scalar(t1_ap, t1_ap, -MAGIC, None, ALU.add), reads=[kt1], writes=[kt1])
        mk.op("dve", lambda e: e.scalar_tensor_tensor(t2_ap, t1_ap, -TWO_PI, ang_ap, ALU.mult, ALU.add), reads=[kt1, kang], writes=[kt2])
        mk.op("dve", lambda e: e.tensor_scalar(t2_ap, t2_ap, -math.pi, math.pi, ALU.max, ALU.min), reads=[kt2], writes=[kt2])
        mk.op("act", lambda e: e.activation(out_ap, t2_ap, AF.Sin), reads=[kt2], writes=[kout])

    def compute_mod_gen(l):
        load_vec(bmod[:], "bmod", b_mod_d.ap()[l].rearrange("(c p) -> c p", p=128), 24)
        load_vec(ng[:], "ng", norm_g_d.ap()[l].rearrange("(c p) -> c p", p=128), 8)
        w2d = w_mod_d.ap()[l]
        nxt = load_w(w2d, 8, 0, 256)
        yield
        for blk in range(12):
            wv, wk = nxt
            ps, _, pk = getps()
            for oc in range(2):
                for kt in range(8):
                    mk.op("pe", lambda e, oc=oc, kt=kt: e.matmul(ps[:, oc:oc + 1], wv[:, kt, oc * 128:(oc + 1) * 128], scond[:, kt:kt + 1],
                                                               start=(kt == 0), stop=(kt == 7)), reads=[wk, "scond"], writes=[pk])
            c = blk * 2
            mk.op("dve", lambda e, c=c: e.tensor_tensor(modv[:, c:c + 2], ps[:, 0:2], bmod[:, c:c + 2], ALU.add), reads=[pk, "bmod"], writes=["modv"])
            if blk + 1 < 12:
                nxt = load_w(w2d, 8, (blk + 1) * 256, 256)
                yield
        mk.op("dve", lambda e: e.scalar_tensor_tensor(gs[:], modv[:, 8:16], 1.0, ng[:], ALU.add, ALU.mult), reads=["modv", "ng"], writes=["gs"])

    def pump(gen, n=1):
        for _ in range(n):
            try:
                next(gen)
            except StopIteration:
                return False
        return True

    def drain(gen):
        for _ in gen:
            pass

    def norm_tile(xt, xk, tmpn, tk, scale_ap, shift_ap, keys, out_fn):
        sq, rstd_b, rs_tmp = NB["sq"], NB["rstd_b"], NB["rs_tmp"]
        mk.op("act", lambda e: e.activation(sq[:], xt[:], AF.Square), reads=[xk], writes=["sq"])
        ps, _, pk = getps()
        for c in range(8):
            mk.op("pe", lambda e, c=c: e.matmul(ps[:], onesD[:], sq[:, c * 512:(c + 1) * 512], start=(c == 0), stop=(c == 7)),
                  reads=["sq", "onesD"], writes=[pk])
        mk.op("act", lambda e: e.activation(rs_tmp[:], ps[:], AF.Sqrt, bias=eps_t[:, 0:1], scale=1.0), reads=[pk, "eps_t"], writes=["rs_tmp"])
        mk.op("dve", lambda e: e.reciprocal(rstd_b[:], rs_tmp[:]), reads=["rs_tmp"], writes=["rstd_b"])
        mk.op("dve", lambda e: e.tensor_tensor(V(tmpn, 4096, 0, [[512, 8], [1, 512]]), V(xt, 4096, 0, [[512, 8], [1, 512]]),
                                               V(rstd_b, 512, 0, [[0, 8], [1, 512]]), ALU.mult), reads=[xk, "rstd_b"], writes=[tk])
        for c in range(8):
            out_fn(c, tmpn[:, c * 512:(c + 1) * 512], scale_ap[:, c:c + 1], shift_ap[:, c:c + 1], [tk] + keys)

    eps_t = sb("eps_t", [128, 1], F32)
    memset("pool", eps_t[:], 1e-6, "eps_t")

    def hT_out(tt):
        def f(c, src, sc, sh, keys):
            mk.op("act", lambda e: e.activation(hT_ap(c, tt * 512, 512), src, AF.Identity, bias=sh, scale=sc),
                  reads=keys, writes=["hT"])
        return f

    def dump(name, ap, key):
        if name in dbg_d:
            mk.dma(dbg_d[name].ap(), ap, reads=[key])

    TB_Bt = [nc.dram_tensor("tb_bt%d" % l_, [128, 8192], BF16, kind="Internal") for l_ in range(DEPTH)]
    TB_GT = [[nc.dram_tensor("tb_gt%d_%d" % (l_, d_), [128, 4096], BF16, kind="Internal") for d_ in range(2)] for l_ in range(DEPTH)]
    TB_Tz = [nc.dram_tensor("tb_tz%d" % l_, [128, 4096], BF16, kind="Internal") for l_ in range(DEPTH)]
    TB_A = [nc.dram_tensor("tb_a%d" % l_, [128, 128], F32, kind="Internal") for l_ in range(DEPTH)]

    def tables_gen(l):
        tg = ExitStack()
        Tz = sb("Tz", [128, 32 * 128], BF16, tg)
        AA = sb("AA", [128, 128], F32, tg)
        joinT = sb("joinT", [128, 1], F32, tg)
        stg = sb("stg", [128, 1024], BF16, tg)
        lam = sb("lam", [128, 2 * 2 * 16], F32, tg)
        apw = sb("apw", [128, 2 * 2 * 16 * 17], F32, tg)
        th = sb("th", [128, 2 * 16], F32, tg)
        lrd = sb("lrd", [128, 2 * 16], F32, tg)
        zoh = sb("zoh", [128, 2 * 2 * 16], F32, tg)
        z1 = sb("z1", [128, 2 * 2 * 16], F32, tg)
        z2 = sb("z2", [128, 2 * 16], F32, tg)
        Bc = sb("Bc", [128, 2 * 16 * 16], F32, tg)
        Cc = sb("Cc", [128, 2 * 16 * 16], F32, tg)
        bbar = sb("bbar", [128, 2 * 2 * 16 * 16], F32, tg)
        Dcol = sb("Dcol", [128, 32], F32, tg)
        tbA = ExitStack()
        ang = sb("ang", [128, 2 * 2 * 16 * 17], F32, tbA)
        sc = sb("sc", [128, 2 * 2 * 16 * 17], F32, tbA)
        tA = sb("tA", [128, 2 * 2 * 16 * 17], F32, tbA)
        tB = sb("tB", [128, 2 * 2 * 16 * 17], F32, tbA)
        mag = sb("mag", [128, 2 * 16 * 17], F32, tbA)
        LR = sb("LR", [32, 8 * 64], F32, tbA)
        DT2 = sb("DT2", [32, 2], F32, tbA)
        CR = sb("CR", [32, 2 * 1024], F32, tbA)
        DTt = sb("DTt", [16, 32], F32, tbA)
        for d in range(2):
            for comp, srcd in enumerate([lre_d, lim_d]):
                mk.dma(LR[:, (d * 2 + comp) * 64:(d * 2 + comp + 1) * 64], srcd.ap()[l, d], writes=["LR"])
        with ncdma():
            mk.dma(DT2[:], ldt_d.ap()[l].rearrange("d g -> g d"), writes=["DT2"])
            mk.dma(DTt[:], AP(s5d_d, l * 512, [[1, 16], [16, 32]]), writes=["DTt"])
            for g2 in range(2):
                for comp, srcd in enumerate([bre_d, bim_d]):
                    mk.dma(V(Bc, 512, comp * 256, [[16, 16], [1, 16]], p0=g2 * 64, np_=64),
                           AP(srcd, (l * 32 + g2) * 1024, [[16, 64], [2048, 16], [1, 16]]), writes=["Bc"])
        for comp, srcd in enumerate([cre_d, cim_d]):
            mk.dma(CR[:, comp * 1024:(comp + 1) * 1024], srcd.ap()[l].rearrange("g c p -> g (c p)"), writes=["CR"])
        mk.op("act", lambda e: e.activation(DT2[:], DT2[:], AF.Exp), reads=["DT2"], writes=["DT2"])
        for d in range(2):
            mk.op("dve", lambda e, d=d: e.tensor_scalar(LR[:, 256 + d * 128: 256 + (d + 1) * 128], LR[:, d * 128:(d + 1) * 128], DT2[:, d:d + 1], None, ALU.mult),
                  reads=["LR", "DT2"], writes=["LR"])
        ps, _, pk = getps()
        for a_ in range(8):
            for g2 in range(2):
                mk.op("pe", lambda e, a_=a_, g2=g2, ps=ps: e.matmul(ps[g2 * 64:(g2 + 1) * 64, a_ * 16:(a_ + 1) * 16], LR[:, a_ * 64:(a_ + 1) * 64], sel2[:, g2 * 16:(g2 + 1) * 16],
                                                                 start=True, stop=True), reads=["LR", "sel2"], writes=[pk])
        mk.op("dve", lambda e, ps=ps: e.tensor_copy(lam[:], ps[:, 0:64]), reads=[pk], writes=["lam"])
        mk.op("dve", lambda e, ps=ps: e.tensor_copy(V(lrd, 32, 0, [[16, 2], [1, 16]]), V(ps, 512, 64, [[32, 2], [1, 16]])), reads=[pk], writes=["lrd"])
        mk.op("dve", lambda e, ps=ps: e.tensor_copy(V(th, 32, 0, [[16, 2], [1, 16]]), V(ps, 512, 80, [[32, 2], [1, 16]])), reads=[pk], writes=["th"])
        for comp in range(2):
            ps, _, pk = getps()
            for c_ in range(16):
                for g2 in range(2):
                    mk.op("pe", lambda e, c_=c_, g2=g2, comp=comp, ps=ps: e.matmul(ps[g2 * 64:(g2 + 1) * 64, c_ * 16:(c_ + 1) * 16], CR[:, comp * 1024 + c_ * 64: comp * 1024 + (c_ + 1) * 64],
                                                                               sel2[:, g2 * 16:(g2 + 1) * 16], start=True, stop=True), reads=["CR", "sel2"], writes=[pk])
            mk.op("dve", lambda e, comp=comp, ps=ps: e.tensor_copy(V(Cc, 512, comp * 256, [[1, 16], [16, 16]]), V(ps, 512, 0, [[16, 16], [1, 16]])), reads=[pk], writes=["Cc"])
        ps, _, pk = getps()
        mk.op("pe", lambda e, ps=ps: e.matmul(ps[:, 0:32], repm[:], DTt[:], start=True, stop=True), reads=["repm", "DTt"], writes=[pk])
        mk.op("dve", lambda e, ps=ps: e.tensor_copy(Dcol[:], ps[:, 0:32]), reads=[pk], writes=["Dcol"])
        for d in range(2):
            base = d * 2 * 272
            base = d * 2 * 272
            mk.op("dve", lambda e, d=d, base=base: e.tensor_tensor(V(ang, 1088, base, [[17, 16], [1, 17]]), V(th, 32, d * 16, [[1, 16], [0, 17]]),
                                                                  V(kk, 17, 0, [[0, 16], [1, 17]]), ALU.mult), reads=["th", "kk"], writes=["ang"])
            mk.op("dve", lambda e, base=base: e.tensor_scalar(ang[:, base + 272: base + 544], ang[:, base: base + 272], math.pi / 2, None, ALU.add),
                  reads=["ang"], writes=["ang"])
            mk.op("dve", lambda e, d=d: e.tensor_tensor(V(mag, 544, d * 272, [[17, 16], [1, 17]]), V(lrd, 32, d * 16, [[1, 16], [0, 17]]),
                                                        V(kk, 17, 0, [[0, 16], [1, 17]]), ALU.mult), reads=["lrd", "kk"], writes=["mag"])
        yield
        sin_rr(sc[:], ang[:], tA[:], tB[:], "sc", "ang", "tA", "tB")
        yield
        mk.op("act", lambda e: e.activation(mag[:], mag[:], AF.Exp), reads=["mag"], writes=["mag"])
        for d in range(2):
            base = d * 2 * 272
            mk.op("dve", lambda e, d=d, base=base: e.tensor_tensor(apw[:, base: base + 272], sc[:, base + 272: base + 544], mag[:, d * 272:(d + 1) * 272], ALU.mult),
                  reads=["sc", "mag"], writes=["apw"])
            mk.op("dve", lambda e, d=d, base=base: e.tensor_tensor(apw[:, base + 272: base + 544], sc[:, base: base + 272], mag[:, d * 272:(d + 1) * 272], ALU.mult),
                  reads=["sc", "mag"], writes=["apw"])
        mk.op("dve", lambda e: e.tensor_copy(joinT[:, 0:1], apw[:, 0:1]), reads=["apw", "Cc", "Dcol", "lam", "lrd", "th", "Bc"], writes=["joinA"])
        yield
        tbA.close()
        tbB = ExitStack()
        BtT = sb("BtT", [128, 16 * 2 * 128], BF16, tbB)
        GTp = sb("GTp", [128, 16 * 2 * 128], BF16, tbB)
        GTc = GTp
        GTpz = sb("GTpz", [128, 4096], BF16, tbB)
        p1 = sb("p1", [128, 1024], F32, tbB)
        p2 = sb("p2", [128, 1024], F32, tbB)
        Ddg = sb("Ddg", [128, 512], F32, tbB)
        tz1 = sb("tz1", [128, 512], F32, tbB)
        def apk(d, comp, k0, kstep, n, extra):
            return V(apw, 1088, (d * 2 + comp) * 272 + 8 + k0, [[17, 16], [kstep, n]] + extra)

        yield
        for d in range(2):
            lr = lam[:, (d * 2) * 16:(d * 2 + 1) * 16]
            li = lam[:, (d * 2 + 1) * 16:(d * 2 + 2) * 16]
            are = V(apw, 1088, (d * 2) * 272 + 9, [[17, 16]])
            aim = V(apw, 1088, (d * 2 + 1) * 272 + 9, [[17, 16]])
            zr = z1[:, 0:16]
            t_ = z1[:, 16:32]
            t2_ = z1[:, 32:48]
            den = z2[:, 0:16]
            mk.op("dve", lambda e, are=are, zr=zr: e.tensor_scalar(zr, are, -1.0, None, ALU.add), reads=["apw"], writes=["z1"])
            mk.op("dve", lambda e, lr=lr, den=den: e.tensor_tensor(den, lr, lr, ALU.mult), reads=["lam"], writes=["z2"])
            mk.op("dve", lambda e, li=li, t_=t_: e.tensor_tensor(t_, li, li, ALU.mult), reads=["lam"], writes=["z1"])
            mk.op("dve", lambda e, den=den, t_=t_: e.tensor_tensor(den, den, t_, ALU.add), reads=["z1", "z2"], writes=["z2"])
            mk.op("dve", lambda e, den=den: e.reciprocal(den, den), reads=["z2"], writes=["z2"])
            zre = zoh[:, (d * 2) * 16:(d * 2 + 1) * 16]
            zim = zoh[:, (d * 2 + 1) * 16:(d * 2 + 2) * 16]
            mk.op("dve", lambda e, zr=zr, lr=lr, t_=t_: e.tensor_tensor(t_, zr, lr, ALU.mult), reads=["z1", "lam"], writes=["z1"])
            mk.op("dve", lambda e, aim=aim, li=li, t2_=t2_: e.tensor_tensor(t2_, aim, li, ALU.mult), reads=["apw", "lam"], writes=["z1"])
            mk.op("dve", lambda e, t_=t_, t2_=t2_: e.tensor_tensor(t_, t_, t2_, ALU.add), reads=["z1"], writes=["z1"])
            mk.op("dve", lambda e, t_=t_, den=den, zre=zre: e.tensor_tensor(zre, t_, den, ALU.mult), reads=["z1", "z2"], writes=["zoh"])
            mk.op("dve", lambda e, aim=aim, lr=lr, t_=t_: e.tensor_tensor(t_, aim, lr, ALU.mult), reads=["apw", "lam"], writes=["z1"])
            mk.op("dve", lambda e, zr=zr, li=li, t2_=t2_: e.tensor_tensor(t2_, zr, li, ALU.mult), reads=["z1", "lam"], writes=["z1"])
            mk.op("dve", lambda e, t_=t_, t2_=t2_: e.tensor_tensor(t_, t_, t2_, ALU.subtract), reads=["z1"], writes=["z1"])
            mk.op("dve", lambda e, t_=t_, den=den, zim=zim: e.tensor_tensor(zim, t_, den, ALU.mult), reads=["z1", "z2"], writes=["zoh"])
        def cmul(out_re, out_im, a_re, a_im, b_re, b_im, shape_n, rk, wk_, neg_im=False, eng="dve", tk=("p1", "p2")):
            q1, q2 = shape_n
            k1, k2 = tk
            mk.op(eng, lambda e: e.tensor_tensor(q1, a_re, b_re, ALU.mult), reads=rk + ["joinA"], writes=[k1])
            mk.op(eng, lambda e: e.tensor_tensor(q2, a_im, b_im, ALU.mult), reads=rk + ["joinA"], writes=[k2])
            mk.op(eng, lambda e: e.tensor_tensor(out_re, q1, q2, ALU.subtract), reads=[k1, k2], writes=wk_)
            mk.op(eng, lambda e: e.tensor_tensor(q1, a_re, b_im, ALU.mult), reads=rk, writes=[k1])
            mk.op(eng, lambda e: e.tensor_tensor(q2, a_im, b_re, ALU.mult), reads=rk, writes=[k2])
            if neg_im and eng == "pool":
                mk.op(eng, lambda e: e.tensor_tensor(q1, q1, q2, ALU.add), reads=[k1, k2], writes=[k1])
                mk.op(eng, lambda e: e.tensor_scalar(out_im, q1, -1.0, None, ALU.mult), reads=[k1], writes=wk_)
            elif neg_im:
                mk.op(eng, lambda e: e.scalar_tensor_tensor(out_im, q1, -1.0, q2, ALU.mult, ALU.subtract), reads=[k1, k2], writes=wk_)
            else:
                mk.op(eng, lambda e: e.tensor_tensor(out_im, q1, q2, ALU.add), reads=[k1, k2], writes=wk_)

        for d in range(2):
            zre_b = V(zoh, 64, (d * 2) * 16, [[1, 16], [0, 16]])
            zim_b = V(zoh, 64, (d * 2 + 1) * 16, [[1, 16], [0, 16]])
            bre = V(Bc, 512, 0, [[16, 16], [1, 16]])
            bim = V(Bc, 512, 256, [[16, 16], [1, 16]])
            obr = V(bbar, 1024, (d * 2) * 256, [[16, 16], [1, 16]])
            obi = V(bbar, 1024, (d * 2 + 1) * 256, [[16, 16], [1, 16]])
            q = (V(p1, 1024, 0, [[16, 16], [1, 16]]), V(p2, 1024, 0, [[16, 16], [1, 16]]))
            cmul(obr, obi, zre_b, zim_b, bre, bim, q, ["zoh", "Bc"], ["bbar"])

        def table(dst, dkey, d, k0, ks, src, skey, srl, soff, neg):
            for jh in range(2):
                q = (V(p1, 1024, 0, [[128, 8], [16, 8], [1, 16]]), V(p2, 1024, 0, [[128, 8], [16, 8], [1, 16]]))
                a_re = V(apw, 1088, (d * 2) * 272 + jh * 8 * 17 + 8 + k0, [[17, 8], [ks, 8], [0, 16]])
                a_im = V(apw, 1088, (d * 2 + 1) * 272 + jh * 8 * 17 + 8 + k0, [[17, 8], [ks, 8], [0, 16]])
                b_re = V(src, srl, soff + jh * 128, [[16, 8], [0, 8], [1, 16]])
                b_im = V(src, srl, soff + 256 + jh * 128, [[16, 8], [0, 8], [1, 16]])
                o_re = V(dst, 4096, jh * 2048, [[256, 8], [16, 8], [1, 16]])
                o_im = V(dst, 4096, jh * 2048 + 128, [[256, 8], [16, 8], [1, 16]])
                cmul(o_re, o_im, a_re, a_im, b_re, b_im, q, ["apw", skey], [dkey], neg_im=neg)
                yield

        for d in range(2):
            k0, ks = (7, -1) if d == 0 else (0, 1)
            yield from table(BtT, "BtT", d, k0, ks, bbar, "bbar", 1024, d * 512, False)
            k0, ks = (1, 1) if d == 0 else (8, -1)
            yield from table(GTc, "GTp", d, k0, ks, Cc, "Cc", 512, 0, True)
            mk.dma(TB_GT[l][d].ap(), GTc[:], reads=["GTp"])
            yield
            k0, ks = (-7, 1) if d == 0 else (0, -1)
            yield from table(GTp, "GTp", d, k0, ks, Cc, "Cc", 512, 0, True)
            for jb in range(4):
                ps, psb, pk = getps()
                for jj in range(4):
                    j = jb * 4 + jj
                    for comp in range(2):
                        col = (jj * 2 + comp) * 128
                        mk.op("pe", lambda e, col=col, j=j, comp=comp, psb=psb: e.transpose(psb[:, col:col + 128], BtT[:, (j * 2 + comp) * 128:(j * 2 + comp + 1) * 128], ident_b[:]),
                              reads=["BtT", "ident_b"], writes=[pk])
                for comp in range(2):
                    mk.op("act", lambda e, comp=comp, psb=psb: e.copy(V(stg, 1024, comp * 64, [[256, 4], [128, 2], [1, 64]]),
                                                                     V(psb, 1024, comp * 128, [[256, 4], [64, 2], [1, 64]])), reads=[pk], writes=["stg"])
                mk.dma(AP(TB_Bt[l], (jb * 8) * 256 + d * 128, [[8192, 128], [256, 8], [1, 128]]), V(stg, 1024, 0, [[128, 8], [1, 128]]), reads=["stg"])
                yield
            for g2 in range(2):
                mk.op("act", lambda e, g2=g2: e.memzero(GTpz[(1 - g2) * 64:(2 - g2) * 64, :]), reads=["joinA"], writes=["GTpz"])
                mk.op("act", lambda e, g2=g2: e.copy(GTpz[g2 * 64:(g2 + 1) * 64, :], GTp[g2 * 64:(g2 + 1) * 64, :]), reads=["GTp"], writes=["GTpz"])
                for gb in range(4):
                    ps, _, pk = getps()
                    for gi in range(4):
                        j = gb * 4 + gi
                        for comp in range(2):
                            lhs = BtT[:, (j * 2 + comp) * 128:(j * 2 + comp + 1) * 128]
                            rhs = GTpz[:, (j * 2 + comp) * 128:(j * 2 + comp + 1) * 128]
                            mk.op("pe", lambda e, gi=gi, lhs=lhs, rhs=rhs, comp=comp, ps=ps: e.matmul(ps[:, gi * 128:(gi + 1) * 128], lhs, rhs, start=(comp == 0), stop=(comp == 1)),
                                  reads=["BtT", "GTpz"], writes=[pk])
                    tzv = V(Tz, 4096, (gb * 8 + g2) * 128, [[256, 4], [1, 128]])
                    mkv = V(maskTz, 256, d * 128, [[0, 4], [1, 128]])
                    mk.op("dve", lambda e, mkv=mkv, ps=ps: e.tensor_tensor(V(tz1, 512, 0, [[128, 4], [1, 128]]), V(ps, 512, 0, [[128, 4], [1, 128]]), mkv, ALU.mult),
                          reads=[pk, "maskTz"], writes=["tz1"])
                    if d == 0:
                        mk.op("dve", lambda e, gb=gb, g2=g2: e.tensor_tensor(V(Ddg, 512, 0, [[128, 4], [1, 128]]), V(ident_f, 128, 0, [[0, 4], [1, 128]]),
                                                                            V(Dcol, 32, gb * 8 + g2, [[2, 4], [0, 128]]), ALU.mult), reads=["ident_f", "Dcol"], writes=["Ddg"])
                        mk.op("dve", lambda e, tzv=tzv: e.tensor_tensor(tzv, V(tz1, 512, 0, [[128, 4], [1, 128]]), V(Ddg, 512, 0, [[128, 4], [1, 128]]), ALU.add), reads=["tz1", "Ddg"], writes=["Tz"])
                    else:
                        mk.op("dve", lambda e, tzv=tzv: e.tensor_tensor(tzv, tzv, V(tz1, 512, 0, [[128, 4], [1, 128]]), ALU.add), reads=["tz1", "Tz"], writes=["Tz"])
                    yield
        for d in range(2):
            a8r = V(apw, 1088, (d * 2) * 272 + 16, [[17, 16]])
            a8i = V(apw, 1088, (d * 2 + 1) * 272 + 16, [[17, 16]])
            for comp in range(2):
                mk.op("pool", lambda e, d=d, comp=comp, a8r=a8r: e.tensor_copy(AA[:, comp * 32 + d * 16: comp * 32 + (d + 1) * 16], a8r), reads=["apw"], writes=["AA"])
            mk.op("pool", lambda e, d=d, a8i=a8i: e.tensor_scalar(AA[:, 64 + d * 16: 64 + (d + 1) * 16], a8i, -1.0, None, ALU.mult), reads=["apw"], writes=["AA"])
            mk.op("pool", lambda e, d=d, a8i=a8i: e.tensor_copy(AA[:, 96 + d * 16: 96 + (d + 1) * 16], a8i), reads=["apw"], writes=["AA"])
        mk.dma(TB_Tz[l].ap(), Tz[:], reads=["Tz"])
        mk.dma(TB_A[l].ap(), AA[:], reads=["AA"])
        yield
        tbB.close()
        tg.close()


    def ckpt(name):
        PHASES.append((name, dict(mk.cnt)))
        if stop == name:
            raise _Stop()

    try:
        modgen = compute_mod_gen(0)
        ckpt("mod")
        with ExitStack() as ph:
            xin = sb("xin", [128, 4 * 1024], F32, ph)
            XT = [sb("xt%d" % i, [128, 8 * 512], F32, ph) for i in range(2)]
            tabgen = tables_gen(0)
            NB["sq"] = sb("sq0", [128, 8 * 512], BF16, ph)
            NB["rstd_b"] = sb("rstd0", [128, 512], F32, ph)
            NB["rs_tmp"] = sb("rstmp0", [128, 512], F32, ph)
            NP = NROW + 64
            pet = sb("pet", [128, 4 * NP], F32, ph)
            pang = sb("pang", [128, 4 * NP], F32, ph)
            pt1 = sb("pt1", [128, 4 * NP], F32, ph)
            pt2 = sb("pt2", [128, 4 * NP], F32, ph)
            pidx_i = sb("pidx_i", [128, 2 + NP], I32, ph)
            pidx = sb("pidx", [128, 2 + NP], F32, ph)
            freq = sb("freq", [128, 2], F32, ph)
            mk.op("pool", lambda e: e.iota(pidx_i[:, 0:2], [[128, 2]], base=0, channel_multiplier=1), writes=["pidx_i"])
            mk.op("pool", lambda e: e.iota(pidx_i[:, 2:2 + NROW], [[1, NROW]], base=0, channel_multiplier=0), writes=["pidx_i"])
            mk.op("pool", lambda e: e.iota(pidx_i[:, 2 + NROW:2 + NP], [[1, 64]], base=0, channel_multiplier=0), writes=["pidx_i"])
            mk.op("pool", lambda e: e.tensor_copy(pidx[:], pidx_i[:]), reads=["pidx_i"], writes=["pidx"])
            mk.op("act", lambda e: e.activation(freq[:], pidx[:, 0:2], AF.Exp, scale=-math.log(10000.0) / 256.0), reads=["pidx"], writes=["freq"])
            for c in range(4):
                mk.op("dve", lambda e, c=c: e.tensor_scalar(pang[:, c * NP:(c + 1) * NP], pidx[:, 2:2 + NP], freq[:, (c % 2):(c % 2) + 1],
                                                            (c // 2) * (math.pi / 2), ALU.mult, ALU.add), reads=["pidx", "freq"], writes=["pang"])
            sin_rr(pet[:], pang[:], pt1[:], pt2[:], "pet", "pang", "pt1", "pt2")
            mk.op("dve", lambda e: e.tensor_scalar(pet[:], pet[:], flags[:, 0:1], None, ALU.mult), reads=["pet", "flags"], writes=["pet"])
            ckpt("pet")
            for tt in range(NTT):
                xt, xk = XT[tt % 2], "xt%d" % (tt % 2)
                for i in range(4):
                    mk.dma(xin[:, i * 1024:(i + 1) * 1024], x_d.ap()[tt * 512 + i * 128: tt * 512 + (i + 1) * 128, :], writes=["xin%d" % i])
                for c in range(8):
                    ps, _, pk = getps()
                    for i in range(4):
                        mk.op("pe", lambda e, c=c, i=i, ps=ps: e.transpose(ps[:, i * 128:(i + 1) * 128], xin[:, i * 1024 + c * 128: i * 1024 + (c + 1) * 128], ident_f[:]),
                              reads=["xin%d" % i, "ident_f"], writes=[pk])
                    if c < 4:
                        pe_ap = V(pet, 4 * NP, c * NP + tt * 8, [[1, 8], [0, 64]])
                    else:
                        pe_ap = V(pet, 4 * NP, (c - 4) * NP + NROW, [[0, 8], [1, 64]])
                    mk.op("dve", lambda e, c=c, pe_ap=pe_ap, ps=ps, xt=xt: e.tensor_tensor(V(xt, 4096, c * 512, [[64, 8], [1, 64]]), V(ps, 512, 0, [[64, 8], [1, 64]]),
                                                                                   pe_ap, ALU.add), reads=[pk, "pet"], writes=[xk])
                    pump(modgen)
                    pump(tabgen, 3)
                mk.dma(V(xs_d, 8 * T, tt * 512, [[T, 8], [1, 512]]), V(xt, 4096, 0, [[512, 8], [1, 512]]), reads=[xk])
                if tt == min(1, NTT - 1):
                    drain(modgen)
                if tt >= 1:
                    norm_tile(XT[(tt - 1) % 2], "xt%d" % ((tt - 1) % 2), XT[(tt - 1) % 2], "xt%d" % ((tt - 1) % 2), gs, modv, ["gs", "modv"], hT_out(tt - 1))
            ckpt("p0load")
            norm_tile(XT[(NTT - 1) % 2], "xt%d" % ((NTT - 1) % 2), XT[(NTT - 1) % 2], "xt%d" % ((NTT - 1) % 2), gs, modv, ["gs", "modv"], hT_out(NTT - 1))
            drain(tabgen)
            ckpt("p0tab")
            mk.barrier()
            ckpt("phase0")

        for l in range(DEPTH):
            w_in_l = w_in_d.ap()[l]
            lay = ExitStack()
            ya = sb("ya", [128, 4 * T], BF16, lay)
            with ExitStack() as ph:
                Bt = sb("Bt", [128, 32 * 2 * 2 * 64], BF16, ph)
                GT = [sb("GT%d" % d, [128, 2 * 4096], BF16, ph) for d in range(2)]
                Tz = sb("Tz", [128, 32 * 128], BF16, ph)
                A1 = sb("A1", [128, 64], F32, ph)
                A2 = sb("A2", [128, 64], F32, ph)
                mk.dma(Bt[:], TB_Bt[l].ap(), writes=["Bt"])
                mk.dma(Tz[:], TB_Tz[l].ap(), writes=["Tz"])
                AAs = sb("AAs", [128, 128], F32, ph)
                mk.dma(AAs[:], TB_A[l].ap(), writes=["AAs"])
                for d in range(2):
                    mk.op("act", lambda e, d=d: e.memzero(GT[d][0:64, 4096:8192]), writes=["GT%d" % d])
                    mk.op("act", lambda e, d=d: e.memzero(GT[d][64:128, 0:4096]), writes=["GT%d" % d])
                    mk.dma(GT[d][0:64, 0:4096], TB_GT[l][d].ap()[0:64, :], writes=["GT%d" % d])
                    mk.dma(GT[d][64:128, 4096:8192], TB_GT[l][d].ap()[64:128, :], writes=["GT%d" % d])
                mk.op("dve", lambda e: e.tensor_copy(A1[:], AAs[:, 0:64]), reads=["AAs"], writes=["A1"])
                mk.op("dve", lambda e: e.tensor_copy(A2[:], AAs[:, 64:128]), reads=["AAs"], writes=["A2"])
                ckpt("tables")
                bufB = sb("bufB", [128, NR * 64], BF16, ph)
                Ut = sb("Ut", [128, 32 * NR], BF16, ph)
                U8 = bufB
                Eb = bufB
                Xb = bufB
                yT = Ut
                for wb in range(2):
                    wv, wk = load_w(w_in_l, 8, wb * 256, 256)
                    for rt in range(RT):
                        for s in range(8):
                            ps, _, pk = getps()
                            for kt in range(8):
                                lhs = V(hT, 8 * T, kt * T + rt * 1024 + s, [[8, 128]])
                                mk.op("pe", lambda e, lhs=lhs, kt=kt, ps=ps: e.matmul(ps[:, 0:256], lhs, wv[:, kt, :], start=(kt == 0), stop=(kt == 7)),
                                      reads=["hT", wk], writes=[pk])
                            oap = V(U8, NR * 64, rt * 4096 + wb * 2048 + s * 16, [[128, 16], [1, 16]])
                            mk.op("act", lambda e, oap=oap, ps=ps: e.copy(oap, V(ps, 512, 0, [[16, 16], [1, 16]])), reads=[pk], writes=["U8"])
                for rt in range(RT):
                    for gb in range(4):
                        ps, psb, pk = getps()
                        for gi in range(8):
                            g = gb * 8 + gi
                            src = U8[:, rt * 4096 + g * 128: rt * 4096 + (g + 1) * 128]
                            mk.op("pe", lambda e, gi=gi, src=src, psb=psb: e.transpose(psb[:, gi * 128:(gi + 1) * 128], src, ident_b[:]),
                                  reads=["U8", "ident_b"], writes=[pk])
                        mk.op("dve", lambda e, gb=gb, rt=rt, psb=psb: e.tensor_copy(V(Ut, 32 * NR, gb * 8 * NR + rt * 128, [[NR, 8], [1, 128]]),
                                                                                   V(psb, 1024, 0, [[128, 8], [1, 128]])), reads=[pk], writes=["Ut%d" % gb])
                dump("Ut", Ut[:], "Ut0")
                ckpt("U")
                for d in range(2):
                    for j in range(16):
                        ps, _, pk = getps()
                        for g2 in range(2):
                            g = 2 * j + g2
                            for comp in range(2):
                                bi = ((g * 2 + d) * 2 + comp) * 64
                                mk.op("pe", lambda e, g2=g2, comp=comp, bi=bi, g=g, ps=ps: e.matmul(ps[g2 * 64:(g2 + 1) * 64, comp * 256: comp * 256 + NR], Bt[:, bi:bi + 64],
                                                                                           Ut[:, g * NR:(g + 1) * NR], start=True, stop=True),
                                      reads=["Bt", "Ut%d" % (g // 8)], writes=[pk])
                        if d == 0:
                            oap = V(Xb, NR * 64, j * NR, [[32 * NR, 2], [1, NR]])
                        else:
                            oap = V(Xb, NR * 64, (16 + j) * NR + NR - 1, [[32 * NR, 2], [-1, NR]])
                        if j % 2 == 0:
                            mk.op("act", lambda e, oap=oap, ps=ps: e.copy(oap, V(ps, 512, 0, [[256, 2], [1, NR]])), reads=[pk], writes=["Xb"])
                        else:
                            mk.op("dve", lambda e, oap=oap, ps=ps: e.tensor_copy(oap, V(ps, 512, 0, [[256, 2], [1, NR]])), reads=[pk], writes=["Xb"])
                mk.barrier()
                dump("Xb", Xb[:], "Xb")
                ckpt("X")
                E2 = sb("E2", [128, 128], F32, ph)
                T1 = sb("T1", [128, 64], F32, ph)
                T2 = sb("T2", [128, 64], F32, ph)
                EX = sb("EX", [128, NSEG * 64], F32, ph)
                OST = sb("OST", [128, NSEG * 64], F32, ph)
                mk.dma(EX[:], s5ext_d.ap()[l], writes=["EX"])
                memset("dve", E2[:], 0.0, "E2_0")
                mk.bufs["E2_1"] = {"w": mk.bufs["E2_0"]["w"], "r": {}}
                RL = NR * 64
                sgtA = [sb("sgtA%d" % i_, [128, 512], BF16, ph) for i_ in range(2)]

                def gate_a_gen():
                    it = 0
                    for wb in range(2):
                        wv, wk = load_w(w_in_l, 8, 512 + wb * 256, 256)
                        for oc2 in range(2):
                            oc = wb * 2 + oc2
                            for tt in range(NTT):
                                ps, _, pk = getps()
                                for kt in range(8):
                                    mk.op("pe", lambda e, ps=ps, oc2=oc2, kt=kt, tt=tt, wv=wv: e.matmul(ps[:], wv[:, kt, oc2 * 128:(oc2 + 1) * 128], hT_ap(kt, tt * 512, 512),
                                                                                               start=(kt == 0), stop=(kt == 7)), reads=[wk, "hT"], writes=[pk])
                                o = oc * T + tt * 512
                                mk.op("act", lambda e, ps=ps, o=o: e.activation(ya[:, o:o + 512], ps[:], AF.Silu), reads=[pk], writes=["ya%d" % oc])
                                yield

                gagen = gate_a_gen()
                for i in range(NR):
                    if i % max(1, NR // 16) == 0:
                        pump(gagen)
                    kb = i // 32
                    co = (i % 2) * 64
                    no = ((i + 1) % 2) * 64
                    ck, nk = "E2_%d" % (i % 2), "E2_%d" % ((i + 1) % 2)
                    cur = E2[:, co:co + 64]
                    ks = "Xs%d" % i
                    if i % 32 == 0:
                        mk.op("dve", lambda e, cur=cur, kb=kb: e.scalar_tensor_tensor(cur, cur, flags[:, 1:2], EX[:, kb * 64:(kb + 1) * 64], ALU.mult, ALU.add),
                              reads=[ck, "flags", "EX"], writes=[ck])
                    mk.op("pool", lambda e, cur=cur: e.tensor_tensor(T1[:], cur, A1[:], ALU.mult), reads=[ck, "A1"], writes=["T1"])
                    mk.op("dve", lambda e, co=co: e.tensor_tensor(V(T2, 64, 0, [[32, 2], [1, 32]]), V(E2, 128, co + 32, [[-32, 2], [1, 32]]), V(A2, 64, 0, [[32, 2], [1, 32]]), ALU.mult),
                          reads=[ck, "A2"], writes=["T2"])
                    mk.op("dve", lambda e, i=i: e.tensor_tensor(T2[:], T2[:], V(Xb, RL, i, [[NR, 64]]), ALU.add), reads=["T2", ks], writes=["T2"])
                    mk.op("act", lambda e, i=i, cur=cur: e.copy(V(Eb, RL, i, [[NR, 64]]), cur), reads=[ck], writes=[ks])
                    mk.op("dve", lambda e, no=no: e.tensor_tensor(E2[:, no:no + 64], T1[:], T2[:], ALU.add), reads=["T1", "T2"], writes=[nk])
                    if (i + 1) % 32 == 0:
                        mk.op("act", lambda e, kb=kb, no=no: e.copy(OST[:, kb * 64:(kb + 1) * 64], E2[:, no:no + 64]), reads=[nk], writes=["OST"])
                drain(gagen)
                mk.dma(s5o_d.ap()[l], OST[:], reads=["OST"])
                mk.barrier()
                Ebr = Bt
                mk.op("act", lambda e: e.copy(V(Ebr, 8192, 0, [[16 * NR, 2], [NR, 16], [1, NR]]), V(Eb, RL, 16 * NR + NR - 1, [[32 * NR, 2], [NR, 16], [-1, NR]])), reads=["Xb"], writes=["Ebr"])
                dump("Eb", Eb[:], "Eb")
                ckpt("chain")
                y8c = sb("y8c", [128, 1024], BF16, ph)
                g1 = sb("g1", [128, 512], F32, ph)
                g2t = sb("g2t", [128, 512], F32, ph)
                UL = 32 * NR
                for fc in range(4):
                    allbanks = []
                    for rt in range(RT):
                        banks = [getps(), getps()]
                        allbanks.append(banks)
                        for gi in range(8):
                            g = fc * 8 + gi
                            j, g2 = g // 2, g % 2
                            ps, _, pk = banks[gi // 4]
                            col0 = (gi % 4) * 128
                            first = True
                            for d in range(2):
                                for comp in range(2):
                                    if d == 0:
                                        lhs = V(Eb, RL, (comp * 32 + j) * NR + rt * 128, [[1, 128]])
                                    else:
                                        lhs = V(Ebr, 8192, (comp * 16 + j) * NR + rt * 128, [[1, 128]])
                                    rhs = GT[d][:, g2 * 4096 + (j * 2 + comp) * 128: g2 * 4096 + (j * 2 + comp + 1) * 128]
                                    mk.op("pe", lambda e, ps=ps, col0=col0, lhs=lhs, rhs=rhs, first=first: e.matmul(ps[:, col0:col0 + 128], lhs, rhs, start=first, stop=False),
                                          reads=["Eb", "Ebr", "GT%d" % d], writes=[pk])
                                    first = False
                            mk.op("pe", lambda e, ps=ps, col0=col0, g=g, rt=rt: e.matmul(ps[:, col0:col0 + 128], Ut[:, g * NR + rt * 128: g * NR + rt * 128 + 128], Tz[:, g * 128:(g + 1) * 128],
                                                                                    start=False, stop=True), reads=["Ut%d" % fc, "Tz"], writes=[pk])
                    for rt in range(RT):
                        banks = allbanks[rt]
                        for bi, (ps, _, pk) in enumerate(banks):
                            gq, gqk = (g1, "g1") if bi == 0 else (g2t, "g2t")
                            mk.op("act", lambda e, ps=ps, gq=gq: e.activation(gq[:], ps[:], AF.Square), reads=[pk], writes=[gqk])
                            mk.op("dve", lambda e, gq=gq: e.tensor_scalar(gq[:], gq[:], 0.044715, 1.0, ALU.mult, ALU.add), reads=[gqk], writes=[gqk])
                            mk.op("dve", lambda e, ps=ps, gq=gq: e.tensor_tensor(gq[:], gq[:], ps[:], ALU.mult), reads=[gqk, pk], writes=[gqk])
                            mk.op("act", lambda e, gq=gq: e.activation(gq[:], gq[:], AF.Sigmoid, scale=1.5957691216), reads=[gqk], writes=[gqk])
                            oap = V(y8c, 1024, bi * 64, [[16, 4], [128, 8], [1, 16]])
                            mk.op("dve", lambda e, ps=ps, oap=oap, gq=gq: e.tensor_tensor(oap, V(gq, 512, 0, [[128, 4], [16, 8], [1, 16]]), V(ps, 512, 0, [[128, 4], [16, 8], [1, 16]]), ALU.mult),
                                  reads=[gqk, pk], writes=["y8c"])
                        ps, psb, pk = getps()
                        for s_ in range(8):
                            mk.op("pe", lambda e, s_=s_, psb=psb: e.transpose(psb[:, s_ * 128:(s_ + 1) * 128], y8c[:, s_ * 128:(s_ + 1) * 128], ident_b[:]), reads=["y8c", "ident_b"], writes=[pk])
                        mk.op("act", lambda e, fc=fc, rt=rt, psb=psb: e.copy(V(yT, UL, fc * T + rt * 1024, [[1, 8], [8, 128]]), V(psb, 1024, 0, [[128, 8], [1, 128]])),
                              reads=[pk], writes=["Ut%d" % fc])
                dump("yT", yT[:, 0:4 * T], "Ut0")
                ckpt("y")
                bglu = sb("bglu", [128, 4], F32, ph)
                load_vec(bglu[:], "bglu", b_glu_d.ap()[l].rearrange("(c p) -> c p", p=128), 4)
                wv, wk = load_w(w_glu_d.ap()[l], 4, 0, 512)
                for oc in range(4):
                    for tt in range(NTT):
                        ps, _, pk = getps()
                        for kt in range(4):
                            mk.op("pe", lambda e, ps=ps, oc=oc, kt=kt, tt=tt: e.matmul(ps[:], wv[:, kt, oc * 128:(oc + 1) * 128], yT[:, kt * T + tt * 512: kt * T + (tt + 1) * 512],
                                                                                  start=(kt == 0), stop=(kt == 3)), reads=[wk, "Ut0", "Ut1", "Ut2", "Ut3"], writes=[pk])
                        sg_, sgk = sgtA[(oc * NTT + tt) % 2], "sgtA%d" % ((oc * NTT + tt) % 2)
                        mk.op("act", lambda e, ps=ps, oc=oc, sg_=sg_: e.activation(sg_[:], ps[:], AF.Sigmoid, bias=bglu[:, oc:oc + 1], scale=1.0), reads=[pk, "bglu"], writes=[sgk])
                        o = oc * T + tt * 512
                        mk.op("dve", lambda e, o=o, sg_=sg_: e.tensor_tensor(sg_[:], yT[:, o:o + 512], sg_[:], ALU.mult), reads=["Ut%d" % oc, sgk], writes=[sgk])
                        mk.op("dve", lambda e, o=o, sg_=sg_: e.tensor_tensor(ya[:, o:o + 512], ya[:, o:o + 512], sg_[:], ALU.mult), reads=["ya%d" % oc, sgk], writes=["ya%d" % oc])
                dump("ya", ya[:], "ya0")
                mk.barrier()
                ckpt("s5")

            yb = sb("yb", [128, 4 * T], BF16, lay)
            with ExitStack() as ph:
                GL = sb("GL", [128, T], BF16, ph)
                wgu = sb("wgu", [16, 512], BF16, ph)
                nbg = sb("nbg", [128, 4], F32, ph)
                gng = sb("gng", [128, 4], F32, ph)
                QF = sb("QF", [128, T], BF16, ph)
                KF = sb("KF", [128, T], BF16, ph)
                VT = sb("VT", [128, NT * 256], BF16, ph)
                LSPd = [sb("LSP%d" % i_, [128, T], F32, ph) for i_ in range(2)]
                CSd = [sb("CS%d" % i_, [128, T], F32, ph) for i_ in range(2)]
                TOTd = [sb("TOT%d" % i_, [128, NT], F32, ph) for i_ in range(2)]
                EBL = [sb("EBL%d" % d, [128, NT], F32, ph) for d in range(2)]
                QE = [sb("QE%d" % d, [128, 2 * T], BF16, ph) for d in range(2)]
                KE = [sb("KE%d" % d, [128, T], BF16, ph) for d in range(2)]
                KET = [sb("KET%d" % d, [128, NT * 128], BF16, ph) for d in range(2)]
                SH = [sb("SH%d" % d, [128, NT * 128], BF16, ph) for d in range(2)]
                PM = [sb("Pm%d" % i, [128, 512], BF16, ph) for i in range(4)]
                TS = [sb("Tst%d" % i, [128, 128], F32, ph) for i in range(4)]
                SX = [sb("Sx%d" % i, [128, 128], F32, ph) for i in range(4)]
                GEXT = [sb("gext%d" % i, [128, 128], F32, ph) for i in range(2)]
                rr = {"pm": 0, "sx": 0, "on": 0}
                OSQ = [sb("osq%d" % i, [128, 512], BF16, ph) for i in range(1)]
                ORS = [sb("ors%d" % i, [128, 512], F32, ph) for i in range(1)]
                with ncdma():
                    mk.dma(ORS[0][0:16, :].rearrange("r (d n) -> r d n", d=2), wgu_d.ap()[l].rearrange("d r n -> r d n"), writes=["ors0"])
                    pass
                load_vec(nbg[:], "nbg", bg_d.ap()[l].rearrange("d (c p) -> (d c) p", p=128), 4)
                load_vec(gng[:], "gng", gng_d.ap()[l].rearrange("(c p) -> c p", p=128), 4)
                mk.op("pool", lambda e: e.tensor_copy(wgu[:], ORS[0][0:16, :]), reads=["ors0"], writes=["wgu"])
                mk.op("dve", lambda e: e.tensor_scalar(nbg[:], nbg[:], -1.0, None, ALU.mult), reads=["nbg"], writes=["nbg"])
                wv, wk = load_w(w_in_l, 8, 2560, 16)
                for tt in range(NTT):
                    ps, _, pk = getps()
                    for kt in range(8):
                        mk.op("pe", lambda e, ps=ps, kt=kt, tt=tt: e.matmul(ps[0:16, :], wv[:, kt, :], hT_ap(kt, tt * 512, 512), start=(kt == 0), stop=(kt == 7)),
                              reads=[wk, "hT"], writes=[pk])
                    mk.op("act", lambda e, ps=ps, tt=tt: e.copy(GL[0:16, tt * 512:(tt + 1) * 512], ps[0:16, :]), reads=[pk], writes=["GL"])
                ckpt("g_gl")
                for c in range(2):
                    for which, dst, col in (("q", QF, 1024 + c * 128), ("k", KF, 1280 + c * 128)):
                        wv, wk = load_w(w_in_l, 8, col, 128)
                        for tt in range(NTT):
                            ps, _, pk = getps()
                            for kt in range(8):
                                mk.op("pe", lambda e, ps=ps, kt=kt, tt=tt, wv=wv: e.matmul(ps[:], wv[:, kt, :], hT_ap(kt, tt * 512, 512), start=(kt == 0), stop=(kt == 7)),
                                      reads=[wk, "hT"], writes=[pk])
                            if which == "q":
                                mk.op("act", lambda e, ps=ps, tt=tt: e.mul(QF[:, tt * 512:(tt + 1) * 512], ps[:], 0.125), reads=[pk], writes=["QF"])
                            else:
                                mk.op("act", lambda e, ps=ps, tt=tt: e.copy(KF[:, tt * 512:(tt + 1) * 512], ps[:]), reads=[pk], writes=["KF"])
                    wv, wk = load_w(w_in_l, 8, 1536 + c * 256, 256)
                    for n in range(NT):
                        ps, _, pk = getps()
                        for kt in range(8):
                            mk.op("pe", lambda e, ps=ps, kt=kt, n=n, wv=wv: e.matmul(ps[:, 0:256], hT_ap(kt, n * 128, 128), wv[:, kt, :], start=(kt == 0), stop=(kt == 7)),
                                  reads=[wk, "hT"], writes=[pk])
                        mk.op("act", lambda e, ps=ps, n=n: e.copy(VT[:, n * 256:(n + 1) * 256], ps[:, 0:256]), reads=[pk], writes=["VT"])
                    ckpt("g_qkv%d" % c)
                    def prep_gen(d):
                        LSP, CS, TOT = LSPd[d], CSd[d], TOTd[d]
                        kl, kc, kt_ = "LSP%d" % d, "CS%d" % d, "TOT%d" % d
                        for tt in range(NTT):
                            ps, _, pk = getps()
                            mk.op("pe", lambda e, ps=ps, tt=tt: e.matmul(ps[:], wgu[0:16, d * 256 + c * 128: d * 256 + (c + 1) * 128], GL[0:16, tt * 512:(tt + 1) * 512], start=True, stop=True),
                                  reads=["wgu", "GL"], writes=[pk])
                            mk.op("act", lambda e, ps=ps, tt=tt: e.activation(LSP[:, tt * 512:(tt + 1) * 512], ps[:], AF.Exp, bias=nbg[:, d * 2 + c: d * 2 + c + 1], scale=-1.0),
                                  reads=[pk, "nbg"], writes=[kl])
                            yield
                        mk.op("act", lambda e: e.activation(LSP[:], LSP[:], AF.Ln, bias=ones_f[:, 0:1], scale=1.0), reads=[kl, "ones_f"], writes=[kl])
                        mk.op("act", lambda e: e.memzero(QE[d][64:128, 0:T]), writes=["QE%d" % d])
                        mk.op("act", lambda e: e.memzero(QE[d][0:64, T:2 * T]), writes=["QE%d" % d])
                        yield
                        mk.op("dve", lambda e: e.tensor_tensor_scan(CS[:], maskc[:], LSP[:], 0.0, ALU.mult, ALU.add), reads=["maskc", kl], writes=[kc])
                        yield
                        mk.op("dve", lambda e: e.tensor_copy(TOT[:], V(CS, T, 127, [[128, NT]])), reads=[kc], writes=[kt_])
                        if d == 1:
                            mk.op("dve", lambda e: e.tensor_tensor(V(CS, T, 0, [[128, NT], [1, 128]]), V(TOT, NT, 0, [[1, NT], [0, 128]]), V(CS, T, 0, [[128, NT], [1, 128]]), ALU.subtract),
                                  reads=[kt_, kc], writes=[kc])
                            yield
                            mk.op("dve", lambda e: e.tensor_tensor(CS[:], CS[:], LSP[:], ALU.add), reads=[kc, kl], writes=[kc])
                        mk.op("act", lambda e: e.activation(EBL[d][:], TOT[:], AF.Exp, scale=-1.0 / 16.0), reads=[kt_], writes=["EBL%d" % d])
                        yield
                        mk.op("act", lambda e: e.activation(LSP[:], CS[:], AF.Exp, scale=-1.0 / 16.0), reads=[kc], writes=[kl])
                        yield
                        mk.op("dve", lambda e: e.tensor_tensor(QE[d][0:64, 0:T], QF[0:64, :], LSP[0:64, :], ALU.mult), reads=["QF", kl], writes=["QE%d" % d])
                        mk.op("dve", lambda e: e.tensor_tensor(QE[d][64:128, T:2 * T], QF[64:128, :], LSP[64:128, :], ALU.mult), reads=["QF", kl], writes=["QE%d" % d])
                        yield
                        mk.op("act", lambda e: e.activation(LSP[:], CS[:], AF.Exp, scale=1.0 / 16.0), reads=[kc], writes=[kl])
                        yield
                        mk.op("dve", lambda e: e.tensor_tensor(KE[d][:], KF[:], LSP[:], ALU.mult), reads=["KF", kl], writes=["KE%d" % d])
                        yield
                        for nb in range(NT // 8):
                            ps, psb, pk = getps()
                            for ni in range(8):
                                n = nb * 8 + ni
                                mk.op("pe", lambda e, psb=psb, ni=ni, n=n: e.transpose(psb[:, ni * 128:(ni + 1) * 128], KE[d][:, n * 128:(n + 1) * 128], ident_b[:]),
                                      reads=["KE%d" % d, "ident_b"], writes=[pk])
                            mk.op("dve", lambda e, psb=psb, nb=nb: e.tensor_copy(KET[d][:, nb * 1024:(nb + 1) * 1024], psb[:, 0:1024]), reads=[pk], writes=["KET%d" % d])
                            yield

                    pg = [prep_gen(0), prep_gen(1)]
                    alive = [True, True]
                    while any(alive):
                        for d in range(2):
                            if alive[d]:
                                alive[d] = pump(pg[d])
                    ckpt("g_prep%d" % c)
                    for d in range(2):
                        mk.dma(GEXT[d][:], glaext_d.ap()[l, d, c], writes=["gext%d" % d])
                    pe_prev = [None, None]
                    for k in range(NT):
                        for d in range(2):
                            gext, gk = GEXT[d], "gext%d" % d
                            pe_ = pe_prev[d]
                            n = k if d == 0 else NT - 1 - k
                            ps, _, pk = getps()
                            for h2 in range(2):
                                mk.op("pe", lambda e, ps=ps, h2=h2, n=n, d=d: e.matmul(ps[h2 * 64:(h2 + 1) * 64, 0:128], KET[d][:, n * 128 + h2 * 64: n * 128 + (h2 + 1) * 64],
                                                                                  VT[:, n * 256 + h2 * 128: n * 256 + (h2 + 1) * 128], start=True, stop=True),
                                      reads=["KET%d" % d, "VT"], writes=[pk])
                            Tc, Tn = TS[d * 2 + k % 2], TS[d * 2 + (k + 1) % 2]
                            tck, tnk = "Tst%d" % (d * 2 + k % 2), "Tst%d" % (d * 2 + (k + 1) % 2)
                            if k == 0:
                                mk.op("act", lambda e, n=n, d=d, gext=gext: e.copy(SH[d][:, n * 128:(n + 1) * 128], gext[:]), reads=[gk], writes=["SH%d" % d])
                                mk.op("dve", lambda e, ps=ps, Tn=Tn, gext=gext: e.tensor_tensor(Tn[:], ps[:, 0:128], gext[:], ALU.add), reads=[pk, gk], writes=[tnk])
                            elif k % 2 == 0:
                                si = rr["sx"]; rr["sx"] = (si + 1) % 4
                                Sx, sk = SX[si], "Sx%d" % si
                                mk.op("dve", lambda e, pe_=pe_, Sx=Sx, Tc=Tc: e.tensor_scalar(Sx[:], Tc[:], pe_, flags[:, 1:2], ALU.mult, ALU.mult), reads=[tck, "EBL%d" % d, "flags"], writes=[sk])
                                mk.op("act", lambda e, n=n, d=d, Sx=Sx: e.copy(SH[d][:, n * 128:(n + 1) * 128], Sx[:]), reads=[sk], writes=["SH%d" % d])
                                mk.op("dve", lambda e, ps=ps, Sx=Sx, Tn=Tn: e.tensor_tensor(Tn[:], ps[:, 0:128], Sx[:], ALU.add), reads=[pk, sk], writes=[tnk])
                            else:
                                mk.op("act", lambda e, n=n, d=d, pe_=pe_, Tc=Tc: e.activation(SH[d][:, n * 128:(n + 1) * 128], Tc[:], AF.Copy, scale=pe_), reads=[tck, "EBL%d" % d], writes=["SH%d" % d])
                                mk.op("dve", lambda e, ps=ps, pe_=pe_, Tc=Tc, Tn=Tn: e.scalar_tensor_tensor(Tn[:], Tc[:], pe_, ps[:, 0:128], ALU.mult, ALU.add), reads=[tck, "EBL%d" % d, pk], writes=[tnk])
                            pe_ = EBL[d][:, n:n + 1]
                            pe_prev[d] = pe_
                            if k % 2 == 1:
                                si = rr["sx"]; rr["sx"] = (si + 1) % 4
                                Sx, sk = SX[si], "Sx%d" % si
                                mk.op("pool", lambda e, pe_=pe_, Sx=Sx, Tn=Tn: e.tensor_scalar(Sx[:], Tn[:], pe_, None, ALU.mult), reads=[tnk, "EBL%d" % d], writes=[sk])
                                mk.dma(glao_d.ap()[l, k // 2, d, c], Sx[:], reads=[sk])
                    ckpt("g_chain%d" % c)
                    items = [(h2, tt) for h2 in range(2) for tt in range(NTT)]

                    def emit_scores(h2, tt):
                        banks = []
                        for half in range(2):
                            ps, _, pk = getps()
                            for nl in range(2):
                                n = tt * 4 + half * 2 + nl
                                for d in range(2):
                                    col = (nl * 2 + d) * 128
                                    mk.op("pe", lambda e, ps=ps, d=d, n=n, h2=h2, col=col: e.matmul(ps[:, col:col + 128], KE[d][:, n * 128:(n + 1) * 128],
                                                                                             QE[d][:, h2 * T + n * 128: h2 * T + (n + 1) * 128], start=True, stop=True),
                                          reads=["KE%d" % d, "QE%d" % d], writes=[pk])
                            pi = rr["pm"]; rr["pm"] = (pi + 1) % 4
                            Pm, pmk = PM[pi], "Pm%d" % pi
                            mk.op("dve", lambda e, ps=ps, Pm=Pm: e.tensor_tensor(Pm[:], ps[:], mask4[:], ALU.mult), reads=[pk, "mask4"], writes=[pmk])
                            banks.append((Pm, pmk))
                        return banks

                    pend = emit_scores(*items[0])
                    for idx, (h2, tt) in enumerate(items):
                        h = 2 * c + h2
                        cur_b = pend
                        if idx + 1 < len(items):
                            pend = emit_scores(*items[idx + 1])
                        pso, _, pko = getps()
                        for ni in range(4):
                            n = tt * 4 + ni
                            Pm, pmk = cur_b[ni // 2]
                            for d in range(2):
                                pc = ((ni % 2) * 2 + d) * 128
                                mk.op("pe", lambda e, pso=pso, d=d, n=n, h2=h2, ni=ni, Pm=Pm, pc=pc: e.matmul(pso[:, ni * 128:(ni + 1) * 128], VT[:, n * 256 + h2 * 128: n * 256 + (h2 + 1) * 128],
                                                                                                   Pm[:, pc:pc + 128], start=(d == 0), stop=False), reads=["VT", pmk], writes=[pko])
                                mk.op("pe", lambda e, pso=pso, d=d, n=n, h2=h2, ni=ni: e.matmul(pso[:, ni * 128:(ni + 1) * 128], SH[d][:, n * 128:(n + 1) * 128],
                                                                                        QE[d][:, h2 * T + n * 128: h2 * T + (n + 1) * 128], start=False, stop=(d == 1)),
                                      reads=["SH%d" % d, "QE%d" % d], writes=[pko])
                        oi = 0
                        osq, ors, osk, ork = OSQ[oi], ORS[oi], "osq%d" % oi, "ors%d" % oi
                        mk.op("act", lambda e, pso=pso, osq=osq: e.activation(osq[:], pso[:], AF.Square), reads=[pko], writes=[osk])
                        ps2, _, pk2 = getps()
                        mk.op("pe", lambda e, ps2=ps2, osq=osq: e.matmul(ps2[:], onesH[:], osq[:], start=True, stop=True), reads=["onesH", osk], writes=[pk2])
                        mk.op("act", lambda e, ps2=ps2, ors=ors: e.activation(ors[:], ps2[:], AF.Sqrt, bias=eps_t[:, 0:1], scale=1.0), reads=[pk2, "eps_t"], writes=[ork])
                        mk.op("dve", lambda e, ors=ors: e.reciprocal(ors[:], ors[:]), reads=[ork], writes=[ork])
                        o = h * T + tt * 512
                        mk.op("dve", lambda e, pso=pso, o=o, h=h, ors=ors: e.scalar_tensor_tensor(yb[:, o:o + 512], pso[:], gng[:, h:h + 1], ors[:], ALU.mult, ALU.mult),
                              reads=[pko, "gng", ork], writes=["yb"])
                ckpt("g_out")
                dump("yb0", yb[:], "yb")
                sgt = OSQ[0]
                for wb in range(2):
                    wv, wk = load_w(w_in_l, 8, 2048 + wb * 256, 256)
                    for oc2 in range(2):
                        oc = wb * 2 + oc2
                        for tt in range(NTT):
                            ps, _, pk = getps()
                            for kt in range(8):
                                mk.op("pe", lambda e, ps=ps, oc2=oc2, kt=kt, tt=tt, wv=wv: e.matmul(ps[:], wv[:, kt, oc2 * 128:(oc2 + 1) * 128], hT_ap(kt, tt * 512, 512),
                                                                                           start=(kt == 0), stop=(kt == 7)), reads=[wk, "hT"], writes=[pk])
                            mk.op("act", lambda e, ps=ps: e.activation(sgt[:], ps[:], AF.Silu), reads=[pk], writes=["osq0"])
                            o = oc * T + tt * 512
                            mk.op("dve", lambda e, o=o: e.tensor_tensor(yb[:, o:o + 512], yb[:, o:o + 512], sgt[:], ALU.mult), reads=["yb", "osq0"], writes=["yb"])
                dump("yb", yb[:], "yb")
                mk.barrier()
                ckpt("gla")

            with ExitStack() as ph:
                mg = sb("mg", [128, 8 * T], BF16, ph)
                gate_l = sb("gate_l", [128, 8], F32, ph)
                mk.op("dve", lambda e: e.tensor_copy(gate_l[:], modv[:, 16:24]), reads=["modv"], writes=["gate_l"])
                last = (l == DEPTH - 1)
                modgen = compute_mod_gen(l + 1) if not last else iter(())
                tabgen = tables_gen(l + 1) if not last else iter(())
                sg = sb("sg", [128, 512], F32, ph)
                tAb = sb("tAb", [128, 512], F32, ph)
                for part, (wp_d, ysrc, yk, mcol) in enumerate(((w_pa_d, ya, "ya", 2576), (w_pb_d, yb, "yb", 3600))):
                    for blk in range(4):
                        wp, wpk = load_w(wp_d.ap()[l], 4, blk * 256, 256)
                        wm, wmk = load_w(w_in_l, 8, mcol + blk * 256, 256)
                        for oc2 in range(2):
                            oc = blk * 2 + oc2
                            for tt in range(NTT):
                                ps1, _, pk1 = getps()
                                ps2, _, pk2 = getps()
                                for kt in range(8):
                                    mk.op("pe", lambda e, ps1=ps1, kt=kt, tt=tt, oc2=oc2, wm=wm: e.matmul(ps1[:], wm[:, kt, oc2 * 128:(oc2 + 1) * 128], hT_ap(kt, tt * 512, 512), start=(kt == 0), stop=(kt == 7)),
                                          reads=[wmk, "hT"], writes=[pk1])
                                for kt in range(4):
                                    mk.op("pe", lambda e, ps2=ps2, kt=kt, tt=tt, oc2=oc2, wp=wp, ysrc=ysrc: e.matmul(ps2[:], wp[:, kt, oc2 * 128:(oc2 + 1) * 128], ysrc[:, kt * T + tt * 512: kt * T + (tt + 1) * 512],
                                                                                                        start=(kt == 0), stop=(kt == 3)), reads=[wpk, yk], writes=[pk2])
                                mk.op("act", lambda e, ps1=ps1: e.activation(sg[:], ps1[:], AF.Sigmoid), reads=[pk1], writes=["sg"])
                                o = oc * T + tt * 512
                                if part == 0:
                                    mk.op("dve", lambda e, ps2=ps2, o=o: e.tensor_tensor(mg[:, o:o + 512], sg[:], ps2[:], ALU.mult), reads=["sg", pk2], writes=["mg"])
                                else:
                                    mk.op("dve", lambda e, ps2=ps2: e.tensor_tensor(tAb[:], sg[:], ps2[:], ALU.mult), reads=["sg", pk2], writes=["tAb"])
                                    mk.op("dve", lambda e, o=o: e.tensor_tensor(mg[:, o:o + 512], mg[:, o:o + 512], tAb[:], ALU.add), reads=["mg", "tAb"], writes=["mg"])
                                pump(tabgen, 1)
                            if tt == NTT - 1:
                                pump(modgen)
                dump("mg", mg[:], "mg")
                drain(tabgen)
                mk.barrier()
                ckpt("mg")
                ckpt("m_merge")
                wo = sb("wo", [128, 8 * 1024], BF16, ph)
                for blk in range(4):
                    i = wrr[0]; wrr[0] = (i + 1) % 2
                    stv = WST[i][:, 0:2048].rearrange("p (k n) -> p k n", k=8)
                    src = w_o_d.ap()[l][:, blk * 256:(blk + 1) * 256].rearrange("(k p) n -> p k n", p=128)
                    with ncdma():
                        mk.dma(stv, src, writes=["wst%d" % i])
                    mk.op("act", lambda e, blk=blk, stv=stv: e.copy(V(wo, 8192, blk * 256, [[1024, 8], [1, 256]]), stv), reads=["wst%d" % i], writes=["wo"])
                drain(modgen)
                ckpt("m_wo_mod")
                XO = [sb("xo%d" % i, [128, 8 * 512], F32, ph) for i in range(2)]
                NB["sq"] = sb("sq1", [128, 8 * 512], BF16, ph)
                NB["rstd_b"] = sb("rstd1", [128, 512], F32, ph)
                NB["rs_tmp"] = sb("rstmp1", [128, 512], F32, ph)
                if last:
                    yout = sb("yout", [128, 2 * 1024], F32, ph)
                for tt in range(NTT):
                    xo, xok = XO[tt % 2], "xo%d" % (tt % 2)
                    tmpn = xo
                    mk.dma(V(xo, 4096, 0, [[512, 8], [1, 512]]), V(xs_d, 8 * T, tt * 512, [[T, 8], [1, 512]]), writes=[xok])
                    for oc in range(8):
                        ps, _, pk = getps()
                        for kt in range(8):
                            mk.op("pe", lambda e, ps=ps, kt=kt, oc=oc, tt=tt: e.matmul(ps[:], wo[:, kt * 1024 + oc * 128: kt * 1024 + (oc + 1) * 128], mg[:, kt * T + tt * 512: kt * T + (tt + 1) * 512],
                                                                                  start=(kt == 0), stop=(kt == 7)), reads=["wo", "mg"], writes=[pk])
                        mk.op("dve", lambda e, ps=ps, oc=oc, xo=xo: e.scalar_tensor_tensor(xo[:, oc * 512:(oc + 1) * 512], ps[:], gate_l[:, oc:oc + 1], xo[:, oc * 512:(oc + 1) * 512], ALU.mult, ALU.add),
                              reads=[pk, "gate_l", xok], writes=[xok])
                    if not last:
                        mk.dma(V(xs_d, 8 * T, tt * 512, [[T, 8], [1, 512]]), V(xo, 4096, 0, [[512, 8], [1, 512]]), reads=[xok])
                        norm_tile(xo, xok, tmpn, xok, gs, modv, ["gs", "modv"], hT_out(tt))
                    else:
                        def fin(c, src, sc_, sh_, keys, xok=xok):
                            mk.op("act", lambda e: e.activation(src, src, AF.Copy, scale=sc_), reads=keys, writes=[xok])
                        norm_tile(xo, xok, tmpn, xok, fng, zero8, ["fng"], fin)
                        for i in range(4):
                            for cb in range(2):
                                ps, _, pk = getps()
                                for cc in range(4):
                                    c = cb * 4 + cc
                                    mk.op("pe", lambda e, ps=ps, cc=cc, c=c, i=i, tmpn=tmpn: e.transpose(ps[:, cc * 128:(cc + 1) * 128], tmpn[:, c * 512 + i * 128: c * 512 + (i + 1) * 128], ident_f[:]),
                                          reads=[xok, "ident_f"], writes=[pk])
                                mk.op("act", lambda e, ps=ps, i=i, cb=cb: e.copy(yout[:, (i % 2) * 1024 + cb * 512: (i % 2) * 1024 + (cb + 1) * 512], ps[:]), reads=[pk], writes=["yout%d" % (i % 2)])
                            mk.dma(y_d.ap()[tt * 512 + i * 128: tt * 512 + (i + 1) * 128, :], yout[:, (i % 2) * 1024:(i % 2 + 1) * 1024], reads=["yout%d" % (i % 2)])
                mk.barrier()
            lay.close()


        stopped = False
    except _Stop:
        stopped = True
    mk.finish()
    if not stopped:
        es.close()
    return nc, mk


NSEG_FULL = 8


def make_core_inputs(inputs, core, NSEG=NSEG_FULL):
    f = np.float32
    T = NSEG * 256
    m = {}
    is_sample = core < 4
    if is_sample:
        b = core
        m["x"] = np.ascontiguousarray(inputs["x_sample"][b], dtype=f)
        cond = inputs["c"][b]
    else:
        p0 = (core - 4) * 4
        xp = np.zeros((T, D), f)
        xp[: 4 * 256] = inputs["x_prompt"][p0:p0 + 4].reshape(4 * 256, D)
        m["x"] = xp
        cond = inputs["c_ctx"]
    m["cond"] = np.ascontiguousarray(np.asarray(cond, f).reshape(8, 128).T)
    fl = np.zeros((128, 2), f)
    if is_sample:
        fl[:, 0] = 1.0
        fl[:, 1] = 1.0
    m["flags"] = fl
    s5ext = np.zeros((DEPTH, 128, NSEG, 2, 2, 16), f)
    glaext = np.zeros((DEPTH, 2, 2, 128, 128), f)
    if is_sample:
        for comp, key in enumerate(("state_s5_re", "state_s5_im")):
            st = np.asarray(inputs[key][b], f)
            st = st.reshape(DEPTH, 2, 16, 2, 64)
            s5ext[:, :, 0, comp, :, :] = st.transpose(0, 3, 4, 1, 2).reshape(DEPTH, 128, 2, 16)
        sg = np.asarray(inputs["state_gla"][b], f)
        glaext[:] = sg.reshape(DEPTH, 2, 2, 128, 128)
    m["s5ext"] = np.ascontiguousarray(s5ext.reshape(DEPTH, 128, NSEG * 64))
    m["glaext"] = glaext
    for k in ("norm_g", "w_mod", "b_mod", "w_in", "gla_wg_up", "gla_bg", "gla_norm_g", "s5_lam_re", "s5_lam_im", "s5_log_dt",
              "s5_b_re", "s5_b_im", "s5_c_re", "s5_c_im", "s5_d", "w_glu", "b_glu", "w_pa", "w_pb", "w_o", "final_norm_g"):
        m[k] = np.ascontiguousarray(inputs[k], dtype=f)
    return m


_CACHE = {}


def kernel(**inputs):
    inputs = {k: np.asarray(v) for k, v in inputs.items()}
    if "nc" not in _CACHE:
        plan = build(NSEG_FULL)[1]
        _CACHE["nc"] = build(NSEG_FULL, needed=set(plan.used))[0]
    nc = _CACHE["nc"]
    in_maps = [make_core_inputs(inputs, c) for c in range(8)]
    res = run_bass_kernel_spmd(nc, in_maps, core_ids=list(range(8)))
    r = res.results
    f = np.float32
    y_prompt = np.zeros((16, 256, D), f)
    y_sample = np.zeros((4, 2048, D), f)
    new_re = np.zeros((16, DEPTH, 2, 32, 64), f)
    new_im = np.zeros((16, DEPTH, 2, 32, 64), f)
    new_gla = np.zeros((16, DEPTH, 2, 4, 64, 128), f)
    for c in range(4):
        y_sample[c] = r[c]["y"]
    for c in range(4, 8):
        p0 = (c - 4) * 4
        y_prompt[p0:p0 + 4] = r[c]["y"][: 4 * 256].reshape(4, 256, D)
        s5o = r[c]["s5o"].reshape(DEPTH, 2, 64, NSEG_FULL, 2, 2, 16)
        glao = r[c]["glao"].reshape(DEPTH, NSEG_FULL, 2, 4, 64, 128)
        for q in range(4):
            for d in range(2):
                kb = q if d == 0 else NSEG_FULL - 1 - q
                blk = s5o[:, :, :, kb, :, d, :]
                arr = blk.transpose(0, 3, 4, 1, 2).reshape(DEPTH, 2, 32, 64)
                new_re[p0 + q, :, d] = arr[:, 0]
                new_im[p0 + q, :, d] = arr[:, 1]
                new_gla[p0 + q, :, d] = glao[:, kb, d]
    return (y_prompt, y_sample, new_re, new_im, new_gla)
```

```python
import math
import numpy as np
from contextlib import ExitStack
import concourse.bass as bass
import concourse.mybir as mybir
from concourse.bass_utils import run_bass_kernel_spmd
from concourse.ap import AP

F32 = mybir.dt.float32
BF16 = mybir.dt.bfloat16
I32 = mybir.dt.int32
AF = mybir.ActivationFunctionType
ALU = mybir.AluOpType

D = 1024
DIN = 4624
DEPTH = 2
MAGIC = 12582912.0
TWO_PI = 2.0 * math.pi


class MK:
    def __init__(self, nc, es, needed=None):
        import bisect
        self._bisect = bisect
        self.needed = needed
        self.used = set()
        if needed is not None:
            self.need_sorted = {e: sorted(n for (ee, n) in needed if ee == e) for e in ["pe", "act", "dve", "pool"]}
        self.nc = nc
        self.eng = {"pe": nc.tensor, "act": nc.scalar, "dve": nc.vector, "pool": nc.gpsimd, "sp": nc.sync}
        self.sem = {}
        self.cnt = {}
        self.waited = {e: {} for e in self.eng}
        for e in ["pe", "act", "dve", "pool"]:
            self.sem[e] = es.enter_context(nc.semaphore("s_" + e))
            self.cnt[e] = 0
        self.dma_sems = []
        for i in range(8):
            self.dma_sems.append([es.enter_context(nc.semaphore("s_dma%d" % i)), 0])
        self.dma_rr = 0
        self.bufs = {}
        self.nins = 0
        self.bar_sem = es.enter_context(nc.semaphore("s_bar"))
        self.bar_cnt = 0

    def _wait(self, e, tok):
        sem, val, key = tok[0], tok[1], tok[2]
        if self.waited[e].get(key, 0) >= val:
            return False
        sval = val
        if key in ("pe", "act", "dve", "pool"):
            self.used.add((key, val))
            if self.needed is not None:
                lst = self.need_sorted[key]
                sval = self._bisect.bisect_right(lst, val)
                assert sval > 0 and lst[sval - 1] == val, "wait on a non-signalling instruction"
        self.eng[e].wait_ge(sem, sval)
        self.waited[e][key] = val
        return True

    def _deps(self, reads, writes):
        deps = []
        for b in reads:
            st = self.bufs.setdefault(b, {"w": None, "r": {}})
            if st["w"] is not None:
                deps.append(st["w"])
        for b in writes:
            st = self.bufs.setdefault(b, {"w": None, "r": {}})
            if st["w"] is not None:
                deps.append(st["w"])
            deps.extend(st["r"].values())
        return deps

    def _mark(self, tok, reads, writes):
        for b in reads:
            self.bufs[b]["r"][tok[2]] = tok
        for b in writes:
            self.bufs[b]["w"] = tok
            self.bufs[b]["r"] = {}

    def op(self, e, fn, reads=(), writes=()):
        for tok in self._deps(reads, writes):
            if tok[3] == "pe" and e == "pe":
                continue
            self._wait(e, tok)
        ins = fn(self.eng[e])
        self.cnt[e] += 1
        self.nins += 1
        if self.needed is None or (e, self.cnt[e]) in self.needed:
            ins.then_inc(self.sem[e], 1)
        tok = (self.sem[e], self.cnt[e], e, e)
        self._mark(tok, reads, writes)
        return ins

    def dma(self, out, in_, reads=(), writes=(), q="sp"):
        slot = self.dma_sems[self.dma_rr]
        key = "dma%d" % self.dma_rr
        self.dma_rr = (self.dma_rr + 1) % len(self.dma_sems)
        if slot[1] > 0:
            self._wait(q, (slot[0], slot[1], key))
        for tok in self._deps(reads, writes):
            self._wait(q, tok)
        ins = self.eng[q].dma_start(out=out, in_=in_)
        slot[1] += 16
        self.nins += 1
        ins.then_inc(slot[0], 16)
        tok = (slot[0], slot[1], key, "dma")
        self._mark(tok, reads, writes)
        return tok

    def barrier(self):
        toks = [(self.sem[e], self.cnt[e], e) for e in ["pe", "act", "dve", "pool"] if self.cnt[e]]
        for i, slot in enumerate(self.dma_sems):
            if slot[1]:
                toks.append((slot[0], slot[1], "dma%d" % i))
        n = 0
        for t in toks:
            if self._wait("sp", t):
                n += 1
                if n % 6 == 0:
                    self.eng["sp"].nop()
        self.bar_cnt += 1
        self.eng["sp"].sem_inc(self.bar_sem, 1)
        for e in ["pe", "act", "dve", "pool"]:
            self._wait(e, (self.bar_sem, self.bar_cnt, "bar"))
            for t in toks:
                if self.waited[e].get(t[2], 0) < t[1]:
                    self.waited[e][t[2]] = t[1]
        self.bufs = {}

    def finish(self):
        n = 0
        toks = []
        for i, slot in enumerate(self.dma_sems):
            if slot[1]:
                toks.append((slot[0], slot[1], "dma%d" % i))
        for e in ["pe", "act", "dve", "pool"]:
            if self.cnt[e]:
                toks.append((self.sem[e], self.cnt[e], e))
        for t in toks:
            if self._wait("sp", t):
                n += 1
                if n % 6 == 0:
                    self.eng["sp"].nop()


PHASES = []


class _Stop(Exception):
    pass


def build(NSEG, dbg=(), stop=None, needed=None):
    T = NSEG * 256
    NT = T // 128
    NTT = T // 512
    NR = T // 8
    RT = NR // 128
    NROW = T // 64
    nc = bass.Bass("TRN2", target_bir_lowering=False)
    es = ExitStack()
    mk = MK(nc, es, needed)

    def din(name, shape):
        return nc.dram_tensor(name, list(shape), F32, kind="ExternalInput")

    def dout(name, shape):
        return nc.dram_tensor(name, list(shape), F32, kind="ExternalOutput")

    x_d = din("x", [T, D])
    cond_d = din("cond", [128, 8])
    flags_d = din("flags", [128, 2])
    s5ext_d = din("s5ext", [DEPTH, 128, NSEG * 64])
    glaext_d = din("glaext", [DEPTH, 2, 2, 128, 128])
    norm_g_d = din("norm_g", [DEPTH, D])
    w_mod_d = din("w_mod", [DEPTH, D, 3 * D])
    b_mod_d = din("b_mod", [DEPTH, 3 * D])
    w_in_d = din("w_in", [DEPTH, D, DIN])
    wgu_d = din("gla_wg_up", [DEPTH, 2, 16, 256])
    bg_d = din("gla_bg", [DEPTH, 2, 256])
    gng_d = din("gla_norm_g", [DEPTH, 512])
    lre_d = din("s5_lam_re", [DEPTH, 2, 32, 64])
    lim_d = din("s5_lam_im", [DEPTH, 2, 32, 64])
    ldt_d = din("s5_log_dt", [DEPTH, 2, 32])
    bre_d = din("s5_b_re", [DEPTH, 32, 64, 16])
    bim_d = din("s5_b_im", [DEPTH, 32, 64, 16])
    cre_d = din("s5_c_re", [DEPTH, 32, 16, 64])
    cim_d = din("s5_c_im", [DEPTH, 32, 16, 64])
    s5d_d = din("s5_d", [DEPTH, 512])
    w_glu_d = din("w_glu", [DEPTH, 512, 512])
    b_glu_d = din("b_glu", [DEPTH, 512])
    w_pa_d = din("w_pa", [DEPTH, 512, D])
    w_pb_d = din("w_pb", [DEPTH, 512, D])
    w_o_d = din("w_o", [DEPTH, D, D])
    fng_d = din("final_norm_g", [D])
    y_d = dout("y", [T, D])
    s5o_d = dout("s5o", [DEPTH, 128, NSEG * 64])
    glao_d = dout("glao", [DEPTH, NSEG, 2, 2, 128, 128])
    xs_d = nc.dram_tensor("xs_scr", [128, 8 * T], F32, kind="Internal")
    dbg_d = {k: dout("dbg_" + k, shp) for k, shp in dbg}

    uniq = [0]

    def sb(name, shape, dt, stack=es):
        uniq[0] += 1
        return stack.enter_context(nc.sbuf_tensor("%s_%d" % (name, uniq[0]), list(shape), dt))

    def V(t, rowlen, off, dims, p0=0, np_=128):
        return AP(t, p0 * rowlen + off, [[rowlen, np_]] + [list(d) for d in dims])

    def ncdma():
        return nc.allow_non_contiguous_dma(reason="small param layout loads")

    hT = sb("hT", [128, 8 * T], BF16)
    WST = [sb("wst%d" % i, [128, 2048], F32) for i in range(2)]
    WBF = [sb("wbf%d" % i, [128, 2048], BF16) for i in range(3)]
    ident_f = sb("ident_f", [128, 128], F32)
    ident_b = sb("ident_b", [128, 128], BF16)
    ones_f = sb("ones_f", [128, 128], F32)
    onesD = sb("onesD", [128, 128], BF16)
    onesH = sb("onesH", [128, 128], BF16)
    maskTz = sb("maskTz", [128, 2 * 128], F32)
    mask4 = sb("mask4", [128, 4 * 128], F32)
    maskc = sb("maskc", [128, T], BF16)
    flags = sb("flags_sb", [128, 2], F32)
    condT = sb("condT", [128, 8], F32)
    scond = sb("scond", [128, 8], BF16)
    modv = sb("modv", [128, 24], F32)
    gs = sb("gs", [128, 8], F32)
    ng = sb("ng", [128, 8], F32)
    bmod = sb("bmod", [128, 24], F32)
    fng = sb("fng", [128, 8], F32)
    zero8 = sb("zero8", [128, 8], F32)
    kk = sb("kk", [128, 17], F32)
    NB = {}
    PS = [es.enter_context(nc.psum_tensor("ps%d" % i, [128, 512], F32)) for i in range(8)]
    PSB = [p.bitcast(BF16) for p in PS]
    ps_rr = [0]

    def getps():
        i = ps_rr[0]
        ps_rr[0] = (i + 1) % 8
        return PS[i], PSB[i], "ps%d" % i

    def memset(e, t_ap, val, key):
        mk.op(e, lambda en: en.memset(t_ap, val), writes=[key])

    memset("pool", ones_f[:], 1.0, "ones_f")
    memset("pool", onesD[:], 1.0 / 1024.0, "onesD")
    memset("pool", onesH[:], 1.0 / 128.0, "onesH")
    memset("pool", zero8[:], 0.0, "zero8")
    mk.op("pool", lambda e: e.affine_select(ident_f[:], ones_f[:], [[-1, 128]], ALU.is_equal, 0.0, base=0, channel_multiplier=1),
          reads=["ones_f"], writes=["ident_f"])
    mk.op("pool", lambda e: e.tensor_copy(ident_b[:], ident_f[:]), reads=["ident_f"], writes=["ident_b"])
    mk.op("pool", lambda e: e.affine_select(V(maskTz, 256, 0, [[16, 8], [1, 16]]), V(ones_f, 128, 0, [[16, 8], [1, 16]]),
                                            [[16, 8], [0, 16]], ALU.is_ge, 0.0, base=15, channel_multiplier=-1),
          reads=["ones_f"], writes=["maskTz"])
    mk.op("pool", lambda e: e.affine_select(V(maskTz, 256, 128, [[16, 8], [1, 16]]), V(ones_f, 128, 0, [[16, 8], [1, 16]]),
                                            [[-16, 8], [0, 16]], ALU.is_ge, 0.0, base=0, channel_multiplier=1),
          reads=["ones_f"], writes=["maskTz"])
    for q in range(4):
        if q % 2 == 0:
            mk.op("pool", lambda e, q=q: e.affine_select(mask4[:, q * 128:(q + 1) * 128], ones_f[:], [[1, 128]], ALU.is_ge, 0.0,
                                                        base=0, channel_multiplier=-1), reads=["ones_f"], writes=["mask4"])
        else:
            mk.op("pool", lambda e, q=q: e.affine_select(mask4[:, q * 128:(q + 1) * 128], ones_f[:], [[-1, 128]], ALU.is_ge, 0.0,
                                                        base=0, channel_multiplier=1), reads=["ones_f"], writes=["mask4"])
    for n in range(NT):
        mk.op("pool", lambda e, n=n: e.affine_select(maskc[:, n * 128:(n + 1) * 128], ones_f[:], [[1, 128]], ALU.is_gt, 0.0,
                                                    base=0, channel_multiplier=0), reads=["ones_f"], writes=["maskc"])
    kki = sb("kki", [128, 32], I32)
    mk.op("pool", lambda e: e.iota(kki[:, 0:17], [[1, 17]], base=-8, channel_multiplier=0), writes=["kki"])
    mk.op("pool", lambda e: e.tensor_copy(kk[:], kki[:, 0:17]), reads=["kki"], writes=["kk"])
    vtmp = sb("vtmp", [32, 128], F32)
    sel2 = sb("sel2", [32, 32], F32)
    repm = sb("repm", [16, 128], F32)
    for g2 in range(2):
        mk.op("pool", lambda e, g2=g2: e.affine_select(sel2[:, g2 * 16:(g2 + 1) * 16], ones_f[0:32, 0:16], [[-2, 16]], ALU.is_equal, 0.0,
                                                      base=-g2, channel_multiplier=1), reads=["ones_f"], writes=["sel2"])
    mk.op("pool", lambda e: e.affine_select(V(repm, 128, 0, [[16, 8], [1, 16]], np_=16), V(ones_f, 128, 0, [[16, 8], [1, 16]], np_=16), [[0, 8], [1, 16]], ALU.is_equal, 0.0,
                                            base=0, channel_multiplier=-1), reads=["ones_f"], writes=["repm"])

    def load_vec(dst, dkey, rows_ap, n):
        mk.dma(vtmp[0:n, :], rows_ap, writes=["vtmp"])
        ps, _, pk = getps()
        mk.op("pe", lambda e: e.transpose(ps[:, 0:n], vtmp[0:n, :], ident_f[0:n, 0:n]), reads=["vtmp", "ident_f"], writes=[pk])
        mk.op("dve", lambda e: e.tensor_copy(dst, ps[:, 0:n]), reads=[pk], writes=[dkey])

    mk.dma(flags[:], flags_d.ap(), writes=["flags"])
    mk.dma(condT[:], cond_d.ap(), writes=["condT"])
    load_vec(fng[:], "fng", fng_d.ap().rearrange("(c p) -> c p", p=128), 8)
    mk.op("act", lambda e: e.activation(scond[:], condT[:], AF.Silu), reads=["condT"], writes=["scond"])

    wrr = [0, 0]

    def load_w(w2d, KT, c0, ncols, cast="act"):
        i = wrr[0]; wrr[0] = (i + 1) % 2
        j = wrr[1]; wrr[1] = (j + 1) % 3
        stv = WST[i][:, 0:KT * ncols].rearrange("p (k n) -> p k n", k=KT)
        bfv = WBF[j][:, 0:KT * ncols].rearrange("p (k n) -> p k n", k=KT)
        src = w2d[:, c0:c0 + ncols].rearrange("(k p) n -> p k n", p=128)
        with ncdma():
            mk.dma(stv, src, writes=["wst%d" % i])
        if cast == "act":
            mk.op("act", lambda e: e.copy(bfv, stv), reads=["wst%d" % i], writes=["wbf%d" % j])
        else:
            mk.op(cast, lambda e: e.tensor_copy(bfv, stv), reads=["wst%d" % i], writes=["wbf%d" % j])
        return bfv, "wbf%d" % j

    def hT_ap(kt, t0, n):
        return hT[:, kt * T + t0: kt * T + t0 + n]

    def sin_rr(out_ap, ang_ap, t1_ap, t2_ap, kout, kang, kt1, kt2):
        mk.op("dve", lambda e: e.tensor_scalar(t1_ap, ang_ap, 1.0 / TWO_PI, MAGIC, ALU.mult, ALU.add), reads=[kang], writes=[kt1])
        mk.op("dve", lambda e: e.tensor_scalar(t1_ap, t1_ap, -MAGIC, None, ALU.add), reads=[kt1], writes=[kt1])
        mk.op("dve", lambda e: e.scalar_tensor_tensor(t2_ap, t1_ap, -TWO_PI, ang_ap, ALU.mult, ALU.add), reads=[kt1, kang], writes=[kt2])
        mk.op("dve", lambda e: e.tensor_scalar(t2_ap, t2_ap, -math.pi, math.pi, ALU.max, ALU.min), reads=[kt2], writes=[kt2])
        mk.op("act", lambda e: e.activation(out_ap, t2_ap, AF.Sin), reads=[kt2], writes=[kout])

    def compute_mod_gen(l):
        load_vec(bmod[:], "bmod", b_mod_d.ap()[l].rearrange("(c p) -> c p", p=128), 24)
        load_vec(ng[:], "ng", norm_g_d.ap()[l].rearrange("(c p) -> c p", p=128), 8)
        w2d = w_mod_d.ap()[l]
        nxt = load_w(w2d, 8, 0, 256)
        yield
        for blk in range(12):
            wv, wk = nxt
            ps, _, pk = getps()
            for oc in range(2):
                for kt in range(8):
                    mk.op("pe", lambda e, oc=oc, kt=kt: e.matmul(ps[:, oc:oc + 1], wv[:, kt, oc * 128:(oc + 1) * 128], scond[:, kt:kt + 1],
                                                               start=(kt == 0), stop=(kt == 7)), reads=[wk, "scond"], writes=[pk])
            c = blk * 2
            mk.op("dve", lambda e, c=c: e.tensor_tensor(modv[:, c:c + 2], ps[:, 0:2], bmod[:, c:c + 2], ALU.add), reads=[pk, "bmod"], writes=["modv"])
            if blk + 1 < 12:
                nxt = load_w(w2d, 8, (blk + 1) * 256, 256)
                yield
        mk.op("dve", lambda e: e.scalar_tensor_tensor(gs[:], modv[:, 8:16], 1.0, ng[:], ALU.add, ALU.mult), reads=["modv", "ng"], writes=["gs"])

    def pump(gen, n=1):
        for _ in range(n):
            try:
                next(gen)
            except StopIteration:
                return False
        return True

    def drain(gen):
        for _ in gen:
            pass

    def norm_tile(xt, xk, tmpn, tk, scale_ap, shift_ap, keys, out_fn):
        sq, rstd_b, rs_tmp = NB["sq"], NB["rstd_b"], NB["rs_tmp"]
        mk.op("act", lambda e: e.activation(sq[:], xt[:], AF.Square), reads=[xk], writes=["sq"])
        ps, _, pk = getps()
        for c in range(8):
            mk.op("pe", lambda e, c=c: e.matmul(ps[:], onesD[:], sq[:, c * 512:(c + 1) * 512], start=(c == 0), stop=(c == 7)),
                  reads=["sq", "onesD"], writes=[pk])
        mk.op("act", lambda e: e.activation(rs_tmp[:], ps[:], AF.Sqrt, bias=eps_t[:, 0:1], scale=1.0), reads=[pk, "eps_t"], writes=["rs_tmp"])
        mk.op("dve", lambda e: e.reciprocal(rstd_b[:], rs_tmp[:]), reads=["rs_tmp"], writes=["rstd_b"])
        mk.op("dve", lambda e: e.tensor_tensor(V(tmpn, 4096, 0, [[512, 8], [1, 512]]), V(xt, 4096, 0, [[512, 8], [1, 512]]),
                                               V(rstd_b, 512, 0, [[0, 8], [1, 512]]), ALU.mult), reads=[xk, "rstd_b"], writes=[tk])
        for c in range(8):
            out_fn(c, tmpn[:, c * 512:(c + 1) * 512], scale_ap[:, c:c + 1], shift_ap[:, c:c + 1], [tk] + keys)

    eps_t = sb("eps_t", [128, 1], F32)
    memset("pool", eps_t[:], 1e-6, "eps_t")

    def hT_out(tt):
        def f(c, src, sc, sh, keys):
            mk.op("act", lambda e: e.activation(hT_ap(c, tt * 512, 512), src, AF.Identity, bias=sh, scale=sc),
                  reads=keys, writes=["hT"])
        return f

    def dump(name, ap, key):
        if name in dbg_d:
            mk.dma(dbg_d[name].ap(), ap, reads=[key])

    TB_Bt = [nc.dram_tensor("tb_bt%d" % l_, [128, 8192], BF16, kind="Internal") for l_ in range(DEPTH)]
    TB_GT = [[nc.dram_tensor("tb_gt%d_%d" % (l_, d_), [128, 4096], BF16, kind="Internal") for d_ in range(2)] for l_ in range(DEPTH)]
    TB_Tz = [nc.dram_tensor("tb_tz%d" % l_, [128, 4096], BF16, kind="Internal") for l_ in range(DEPTH)]
    TB_A = [nc.dram_tensor("tb_a%d" % l_, [128, 128], F32, kind="Internal") for l_ in range(DEPTH)]

    def tables_gen(l):
        tg = ExitStack()
        Tz = sb("Tz", [128, 32 * 128], BF16, tg)
        AA = sb("AA", [128, 128], F32, tg)
        joinT = sb("joinT", [128, 1], F32, tg)
        stg = sb("stg", [128, 1024], BF16, tg)
        lam = sb("lam", [128, 2 * 2 * 16], F32, tg)
        apw = sb("apw", [128, 2 * 2 * 16 * 17], F32, tg)
        th = sb("th", [128, 2 * 16], F32, tg)
        lrd = sb("lrd", [128, 2 * 16], F32, tg)
        zoh = sb("zoh", [128, 2 * 2 * 16], F32, tg)
        z1 = sb("z1", [128, 2 * 2 * 16], F32, tg)
        z2 = sb("z2", [128, 2 * 16], F32, tg)
        Bc = sb("Bc", [128, 2 * 16 * 16], F32, tg)
        Cc = sb("Cc", [128, 2 * 16 * 16], F32, tg)
        bbar = sb("bbar", [128, 2 * 2 * 16 * 16], F32, tg)
        Dcol = sb("Dcol", [128, 32], F32, tg)
        tbA = ExitStack()
        ang = sb("ang", [128, 2 * 2 * 16 * 17], F32, tbA)
        sc = sb("sc", [128, 2 * 2 * 16 * 17], F32, tbA)
        tA = sb("tA", [128, 2 * 2 * 16 * 17], F32, tbA)
        tB = sb("tB", [128, 2 * 2 * 16 * 17], F32, tbA)
        mag = sb("mag", [128, 2 * 16 * 17], F32, tbA)
        LR = sb("LR", [32, 8 * 64], F32, tbA)
        DT2 = sb("DT2", [32, 2], F32, tbA)
        CR = sb("CR", [32, 2 * 1024], F32, tbA)
        DTt = sb("DTt", [16, 32], F32, tbA)
        for d in range(2):
            for comp, srcd in enumerate([lre_d, lim_d]):
                mk.dma(LR[:, (d * 2 + comp) * 64:(d * 2 + comp + 1) * 64], srcd.ap()[l, d], writes=["LR"])
        with ncdma():
            mk.dma(DT2[:], ldt_d.ap()[l].rearrange("d g -> g d"), writes=["DT2"])
            mk.dma(DTt[:], AP(s5d_d, l * 512, [[1, 16], [16, 32]]), writes=["DTt"])
            for g2 in range(2):
                for comp, srcd in enumerate([bre_d, bim_d]):
                    mk.dma(V(Bc, 512, comp * 256, [[16, 16], [1, 16]], p0=g2 * 64, np_=64),
                           AP(srcd, (l * 32 + g2) * 1024, [[16, 64], [2048, 16], [1, 16]]), writes=["Bc"])
        for comp, srcd in enumerate([cre_d, cim_d]):
            mk.dma(CR[:, comp * 1024:(comp + 1) * 1024], srcd.ap()[l].rearrange("g c p -> g (c p)"), writes=["CR"])
        mk.op("act", lambda e: e.activation(DT2[:], DT2[:], AF.Exp), reads=["DT2"], writes=["DT2"])
        for d in range(2):
            mk.op("dve", lambda e, d=d: e.tensor_scalar(LR[:, 256 + d * 128: 256 + (d + 1) * 128], LR[:, d * 128:(d + 1) * 128], DT2[:, d:d + 1], None, ALU.mult),
                  reads=["LR", "DT2"], writes=["LR"])
        ps, _, pk = getps()
        for a_ in range(8):
            for g2 in range(2):
                mk.op("pe", lambda e, a_=a_, g2=g2, ps=ps: e.matmul(ps[g2 * 64:(g2 + 1) * 64, a_ * 16:(a_ + 1) * 16], LR[:, a_ * 64:(a_ + 1) * 64], sel2[:, g2 * 16:(g2 + 1) * 16],
                                                                 start=True, stop=True), reads=["LR", "sel2"], writes=[pk])
        mk.op("dve", lambda e, ps=ps: e.tensor_copy(lam[:], ps[:, 0:64]), reads=[pk], writes=["lam"])
        mk.op("dve", lambda e, ps=ps: e.tensor_copy(V(lrd, 32, 0, [[16, 2], [1, 16]]), V(ps, 512, 64, [[32, 2], [1, 16]])), reads=[pk], writes=["lrd"])
        mk.op("dve", lambda e, ps=ps: e.tensor_copy(V(th, 32, 0, [[16, 2], [1, 16]]), V(ps, 512, 80, [[32, 2], [1, 16]])), reads=[pk], writes=["th"])
        for comp in range(2):
            ps, _, pk = getps()
            for c_ in range(16):
                for g2 in range(2):
                    mk.op("pe", lambda e, c_=c_, g2=g2, comp=comp, ps=ps: e.matmul(ps[g2 * 64:(g2 + 1) * 64, c_ * 16:(c_ + 1) * 16], CR[:, comp * 1024 + c_ * 64: comp * 1024 + (c_ + 1) * 64],
                                                                               sel2[:, g2 * 16:(g2 + 1) * 16], start=True, stop=True), reads=["CR", "sel2"], writes=[pk])
            mk.op("dve", lambda e, comp=comp, ps=ps: e.tensor_copy(V(Cc, 512, comp * 256, [[1, 16], [16, 16]]), V(ps, 512, 0, [[16, 16], [1, 16]])), reads=[pk], writes=["Cc"])
        ps, _, pk = getps()
        mk.op("pe", lambda e, ps=ps: e.matmul(ps[:, 0:32], repm[:], DTt[:], start=True, stop=True), reads=["repm", "DTt"], writes=[pk])
        mk.op("dve", lambda e, ps=ps: e.tensor_copy(Dcol[:], ps[:, 0:32]), reads=[pk], writes=["Dcol"])
        for d in range(2):
            base = d * 2 * 272
            base = d * 2 * 272
            mk.op("dve", lambda e, d=d, base=base: e.tensor_tensor(V(ang, 1088, base, [[17, 16], [1, 17]]), V(th, 32, d * 16, [[1, 16], [0, 17]]),
                                                                  V(kk, 17, 0, [[0, 16], [1, 17]]), ALU.mult), reads=["th", "kk"], writes=["ang"])
            mk.op("dve", lambda e, base=base: e.tensor_scalar(ang[:, base + 272: base + 544], ang[:, base: base + 272], math.pi / 2, None, ALU.add),
                  reads=["ang"], writes=["ang"])
            mk.op("dve", lambda e, d=d: e.tensor_tensor(V(mag, 544, d * 272, [[17, 16], [1, 17]]), V(lrd, 32, d * 16, [[1, 16], [0, 17]]),
                                                        V(kk, 17, 0, [[0, 16], [1, 17]]), ALU.mult), reads=["lrd", "kk"], writes=["mag"])
        yield
        sin_rr(sc[:], ang[:], tA[:], tB[:], "sc", "ang", "tA", "tB")
        yield
        mk.op("act", lambda e: e.activation(mag[:], mag[:], AF.Exp), reads=["mag"], writes=["mag"])
        for d in range(2):
            base = d * 2 * 272
            mk.op("dve", lambda e, d=d, base=base: e.tensor_tensor(apw[:, base: base + 272], sc[:, base + 272: base + 544], mag[:, d * 272:(d + 1) * 272], ALU.mult),
                  reads=["sc", "mag"], writes=["apw"])
            mk.op("dve", lambda e, d=d, base=base: e.tensor_tensor(apw[:, base + 272: base + 544], sc[:, base: base + 272], mag[:, d * 272:(d + 1) * 272], ALU.mult),
                  reads=["sc", "mag"], writes=["apw"])
        mk.op("dve", lambda e: e.tensor_copy(joinT[:, 0:1], apw[:, 0:1]), reads=["apw", "Cc", "Dcol", "lam", "lrd", "th", "Bc"], writes=["joinA"])
        yield
        tbA.close()
        tbB = ExitStack()
        BtT = sb("BtT", [128, 16 * 2 * 128], BF16, tbB)
        GTp = sb("GTp", [128, 16 * 2 * 128], BF16, tbB)
        GTc = GTp
        GTpz = sb("GTpz", [128, 4096], BF16, tbB)
        p1 = sb("p1", [128, 1024], F32, tbB)
        p2 = sb("p2", [128, 1024], F32, tbB)
        Ddg = sb("Ddg", [128, 512], F32, tbB)
        tz1 = sb("tz1", [128, 512], F32, tbB)
        def apk(d, comp, k0, kstep, n, extra):
            return V(apw, 1088, (d * 2 + comp) * 272 + 8 + k0, [[17, 16], [kstep, n]] + extra)

        yield
        for d in range(2):
            lr = lam[:, (d * 2) * 16:(d * 2 + 1) * 16]
            li = lam[:, (d * 2 + 1) * 16:(d * 2 + 2) * 16]
            are = V(apw, 1088, (d * 2) * 272 + 9, [[17, 16]])
            aim = V(apw, 1088, (d * 2 + 1) * 272 + 9, [[17, 16]])
            zr = z1[:, 0:16]
            t_ = z1[:, 16:32]
            t2_ = z1[:, 32:48]
            den = z2[:, 0:16]
            mk.op("dve", lambda e, are=are, zr=zr: e.tensor_scalar(zr, are, -1.0, None, ALU.add), reads=["apw"], writes=["z1"])
            mk.op("dve", lambda e, lr=lr, den=den: e.tensor_tensor(den, lr, lr, ALU.mult), reads=["lam"], writes=["z2"])
            mk.op("dve", lambda e, li=li, t_=t_: e.tensor_tensor(t_, li, li, ALU.mult), reads=["lam"], writes=["z1"])
            mk.op("dve", lambda e, den=den, t_=t_: e.tensor_tensor(den, den, t_, ALU.add), reads=["z1", "z2"], writes=["z2"])
            mk.op("dve", lambda e, den=den: e.reciprocal(den, den), reads=["z2"], writes=["z2"])
            zre = zoh[:, (d * 2) * 16:(d * 2 + 1) * 16]
            zim = zoh[:, (d * 2 + 1) * 16:(d * 2 + 2) * 16]
            mk.op("dve", lambda e, zr=zr, lr=lr, t_=t_: e.tensor_tensor(t_, zr, lr, ALU.mult), reads=["z1", "lam"], writes=["z1"])
            mk.op("dve", lambda e, aim=aim, li=li, t2_=t2_: e.tensor_tensor(t2_, aim, li, ALU.mult), reads=["apw", "lam"], writes=["z1"])
            mk.op("dve", lambda e, t_=t_, t2_=t2_: e.tensor_tensor(t_, t_, t2_, ALU.add), reads=["z1"], writes=["z1"])
            mk.op("dve", lambda e, t_=t_, den=den, zre=zre: e.tensor_tensor(zre, t_, den, ALU.mult), reads=["z1", "z2"], writes=["zoh"])
            mk.op("dve", lambda e, aim=aim, lr=lr, t_=t_: e.tensor_tensor(t_, aim, lr, ALU.mult), reads=["apw", "lam"], writes=["z1"])
            mk.op("dve", lambda e, zr=zr, li=li, t2_=t2_: e.tensor_tensor(t2_, zr, li, ALU.mult), reads=["z1", "lam"], writes=["z1"])
            mk.op("dve", lambda e, t_=t_, t2_=t2_: e.tensor_tensor(t_, t_, t2_, ALU.subtract), reads=["z1"], writes=["z1"])
            mk.op("dve", lambda e, t_=t_, den=den, zim=zim: e.tensor_tensor(zim, t_, den, ALU.mult), reads=["z1", "z2"], writes=["zoh"])
        def cmul(out_re, out_im, a_re, a_im, b_re, b_im, shape_n, rk, wk_, neg_im=False, eng="dve", tk=("p1", "p2"), part=None):
            q1, q2 = shape_n
            k1, k2 = tk
            if part != 1:
                mk.op(eng, lambda e: e.tensor_tensor(q1, a_re, b_re, ALU.mult), reads=rk + ["joinA"], writes=[k1])
                mk.op(eng, lambda e: e.tensor_tensor(q2, a_im, b_im, ALU.mult), reads=rk + ["joinA"], writes=[k2])
                mk.op(eng, lambda e: e.tensor_tensor(out_re, q1, q2, ALU.subtract), reads=[k1, k2], writes=wk_)
            if part == 0:
                return
            mk.op(eng, lambda e: e.tensor_tensor(q1, a_re, b_im, ALU.mult), reads=rk, writes=[k1])
            mk.op(eng, lambda e: e.tensor_tensor(q2, a_im, b_re, ALU.mult), reads=rk, writes=[k2])
            if neg_im and eng == "pool":
                mk.op(eng, lambda e: e.tensor_tensor(q1, q1, q2, ALU.add), reads=[k1, k2], writes=[k1])
                mk.op(eng, lambda e: e.tensor_scalar(out_im, q1, -1.0, None, ALU.mult), reads=[k1], writes=wk_)
            elif neg_im:
                mk.op(eng, lambda e: e.scalar_tensor_tensor(out_im, q1, -1.0, q2, ALU.mult, ALU.subtract), reads=[k1, k2], writes=wk_)
            else:
                mk.op(eng, lambda e: e.tensor_tensor(out_im, q1, q2, ALU.add), reads=[k1, k2], writes=wk_)

        for d in range(2):
            zre_b = V(zoh, 64, (d * 2) * 16, [[1, 16], [0, 16]])
            zim_b = V(zoh, 64, (d * 2 + 1) * 16, [[1, 16], [0, 16]])
            bre = V(Bc, 512, 0, [[16, 16], [1, 16]])
            bim = V(Bc, 512, 256, [[16, 16], [1, 16]])
            obr = V(bbar, 1024, (d * 2) * 256, [[16, 16], [1, 16]])
            obi = V(bbar, 1024, (d * 2 + 1) * 256, [[16, 16], [1, 16]])
            q = (V(p1, 1024, 0, [[16, 16], [1, 16]]), V(p2, 1024, 0, [[16, 16], [1, 16]]))
            cmul(obr, obi, zre_b, zim_b, bre, bim, q, ["zoh", "Bc"], ["bbar"])

        def table(dst, dkey, d, k0, ks, src, skey, srl, soff, neg):
            for jh in range(2):
                q = (V(p1, 1024, 0, [[128, 8], [16, 8], [1, 16]]), V(p2, 1024, 0, [[128, 8], [16, 8], [1, 16]]))
                a_re = V(apw, 1088, (d * 2) * 272 + jh * 8 * 17 + 8 + k0, [[17, 8], [ks, 8], [0, 16]])
                a_im = V(apw, 1088, (d * 2 + 1) * 272 + jh * 8 * 17 + 8 + k0, [[17, 8], [ks, 8], [0, 16]])
                b_re = V(src, srl, soff + jh * 128, [[16, 8], [0, 8], [1, 16]])
                b_im = V(src, srl, soff + 256 + jh * 128, [[16, 8], [0, 8], [1, 16]])
                o_re = V(dst, 4096, jh * 2048, [[256, 8], [16, 8], [1, 16]])
                o_im = V(dst, 4096, jh * 2048 + 128, [[256, 8], [16, 8], [1, 16]])
                cmul(o_re, o_im, a_re, a_im, b_re, b_im, q, ["apw", skey], [dkey], neg_im=neg, part=0)
                yield
                cmul(o_re, o_im, a_re, a_im, b_re, b_im, q, ["apw", skey], [dkey], neg_im=neg, part=1)
                yield

        for d in range(2):
            k0, ks = (7, -1) if d == 0 else (0, 1)
            yield from table(BtT, "BtT", d, k0, ks, bbar, "bbar", 1024, d * 512, False)
            k0, ks = (1, 1) if d == 0 else (8, -1)
            yield from table(GTc, "GTp", d, k0, ks, Cc, "Cc", 512, 0, True)
            mk.dma(TB_GT[l][d].ap(), GTc[:], reads=["GTp"])
            yield
            k0, ks = (-7, 1) if d == 0 else (0, -1)
            yield from table(GTp, "GTp", d, k0, ks, Cc, "Cc", 512, 0, True)
            for jb in range(4):
                ps, psb, pk = getps()
                for jj in range(4):
                    j = jb * 4 + jj
                    for comp in range(2):
                        col = (jj * 2 + comp) * 128
                        mk.op("pe", lambda e, col=col, j=j, comp=comp, psb=psb: e.transpose(psb[:, col:col + 128], BtT[:, (j * 2 + comp) * 128:(j * 2 + comp + 1) * 128], ident_b[:]),
                              reads=["BtT", "ident_b"], writes=[pk])
                for comp in range(2):
                    mk.op("act", lambda e, comp=comp, psb=psb: e.copy(V(stg, 1024, comp * 64, [[256, 4], [128, 2], [1, 64]]),
                                                                     V(psb, 1024, comp * 128, [[256, 4], [64, 2], [1, 64]])), reads=[pk], writes=["stg"])
                mk.dma(AP(TB_Bt[l], (jb * 8) * 256 + d * 128, [[8192, 128], [256, 8], [1, 128]]), V(stg, 1024, 0, [[128, 8], [1, 128]]), reads=["stg"])
                yield
            for g2 in range(2):
                mk.op("act", lambda e, g2=g2: e.memzero(GTpz[(1 - g2) * 64:(2 - g2) * 64, :]), reads=["joinA"], writes=["GTpz"])
                mk.op("act", lambda e, g2=g2: e.copy(GTpz[g2 * 64:(g2 + 1) * 64, :], GTp[g2 * 64:(g2 + 1) * 64, :]), reads=["GTp"], writes=["GTpz"])
                for gb in range(4):
                    ps, _, pk = getps()
                    for gi in range(4):
                        j = gb * 4 + gi
                        for comp in range(2):
                            lhs = BtT[:, (j * 2 + comp) * 128:(j * 2 + comp + 1) * 128]
                            rhs = GTpz[:, (j * 2 + comp) * 128:(j * 2 + comp + 1) * 128]
                            mk.op("pe", lambda e, gi=gi, lhs=lhs, rhs=rhs, comp=comp, ps=ps: e.matmul(ps[:, gi * 128:(gi + 1) * 128], lhs, rhs, start=(comp == 0), stop=(comp == 1)),
                                  reads=["BtT", "GTpz"], writes=[pk])
                    tzv = V(Tz, 4096, (gb * 8 + g2) * 128, [[256, 4], [1, 128]])
                    mkv = V(maskTz, 256, d * 128, [[0, 4], [1, 128]])
                    mk.op("dve", lambda e, mkv=mkv, ps=ps: e.tensor_tensor(V(tz1, 512, 0, [[128, 4], [1, 128]]), V(ps, 512, 0, [[128, 4], [1, 128]]), mkv, ALU.mult),
                          reads=[pk, "maskTz"], writes=["tz1"])
                    if d == 0:
                        mk.op("dve", lambda e, gb=gb, g2=g2: e.tensor_tensor(V(Ddg, 512, 0, [[128, 4], [1, 128]]), V(ident_f, 128, 0, [[0, 4], [1, 128]]),
                                                                            V(Dcol, 32, gb * 8 + g2, [[2, 4], [0, 128]]), ALU.mult), reads=["ident_f", "Dcol"], writes=["Ddg"])
                        mk.op("dve", lambda e, tzv=tzv: e.tensor_tensor(tzv, V(tz1, 512, 0, [[128, 4], [1, 128]]), V(Ddg, 512, 0, [[128, 4], [1, 128]]), ALU.add), reads=["tz1", "Ddg"], writes=["Tz"])
                    else:
                        mk.op("dve", lambda e, tzv=tzv: e.tensor_tensor(tzv, tzv, V(tz1, 512, 0, [[128, 4], [1, 128]]), ALU.add), reads=["tz1", "Tz"], writes=["Tz"])
                    yield
        for d in range(2):
            a8r = V(apw, 1088, (d * 2) * 272 + 16, [[17, 16]])
            a8i = V(apw, 1088, (d * 2 + 1) * 272 + 16, [[17, 16]])
            for comp in range(2):
                mk.op("pool", lambda e, d=d, comp=comp, a8r=a8r: e.tensor_copy(AA[:, comp * 32 + d * 16: comp * 32 + (d + 1) * 16], a8r), reads=["apw"], writes=["AA"])
            mk.op("pool", lambda e, d=d, a8i=a8i: e.tensor_scalar(AA[:, 64 + d * 16: 64 + (d + 1) * 16], a8i, -1.0, None, ALU.mult), reads=["apw"], writes=["AA"])
            mk.op("pool", lambda e, d=d, a8i=a8i: e.tensor_copy(AA[:, 96 + d * 16: 96 + (d + 1) * 16], a8i), reads=["apw"], writes=["AA"])
        mk.dma(TB_Tz[l].ap(), Tz[:], reads=["Tz"])
        mk.dma(TB_A[l].ap(), AA[:], reads=["AA"])
        yield
        tbB.close()
        tg.close()


    def ckpt(name):
        PHASES.append((name, dict(mk.cnt)))
        if stop == name:
            raise _Stop()

    try:
        modgen = compute_mod_gen(0)
        ckpt("mod")
        with ExitStack() as ph:
            xin = sb("xin", [128, 4 * 1024], F32, ph)
            XT = [sb("xt%d" % i, [128, 8 * 512], F32, ph) for i in range(2)]
            tabgen = tables_gen(0)
            NB["sq"] = sb("sq0", [128, 8 * 512], BF16, ph)
            NB["rstd_b"] = sb("rstd0", [128, 512], F32, ph)
            NB["rs_tmp"] = sb("rstmp0", [128, 512], F32, ph)
            NP = NROW + 64
            pet = sb("pet", [128, 4 * NP], F32, ph)
            pang = sb("pang", [128, 4 * NP], F32, ph)
            pt1 = sb("pt1", [128, 4 * NP], F32, ph)
            pt2 = sb("pt2", [128, 4 * NP], F32, ph)
            pidx_i = sb("pidx_i", [128, 2 + NP], I32, ph)
            pidx = sb("pidx", [128, 2 + NP], F32, ph)
            freq = sb("freq", [128, 2], F32, ph)
            mk.op("pool", lambda e: e.iota(pidx_i[:, 0:2], [[128, 2]], base=0, channel_multiplier=1), writes=["pidx_i"])
            mk.op("pool", lambda e: e.iota(pidx_i[:, 2:2 + NROW], [[1, NROW]], base=0, channel_multiplier=0), writes=["pidx_i"])
            mk.op("pool", lambda e: e.iota(pidx_i[:, 2 + NROW:2 + NP], [[1, 64]], base=0, channel_multiplier=0), writes=["pidx_i"])
            mk.op("pool", lambda e: e.tensor_copy(pidx[:], pidx_i[:]), reads=["pidx_i"], writes=["pidx"])
            mk.op("act", lambda e: e.activation(freq[:], pidx[:, 0:2], AF.Exp, scale=-math.log(10000.0) / 256.0), reads=["pidx"], writes=["freq"])
            for c in range(4):
                mk.op("dve", lambda e, c=c: e.tensor_scalar(pang[:, c * NP:(c + 1) * NP], pidx[:, 2:2 + NP], freq[:, (c % 2):(c % 2) + 1],
                                                            (c // 2) * (math.pi / 2), ALU.mult, ALU.add), reads=["pidx", "freq"], writes=["pang"])
            sin_rr(pet[:], pang[:], pt1[:], pt2[:], "pet", "pang", "pt1", "pt2")
            mk.op("dve", lambda e: e.tensor_scalar(pet[:], pet[:], flags[:, 0:1], None, ALU.mult), reads=["pet", "flags"], writes=["pet"])
            ckpt("pet")
            for tt in range(NTT):
                xt, xk = XT[tt % 2], "xt%d" % (tt % 2)
                for i in range(4):
                    mk.dma(xin[:, i * 1024:(i + 1) * 1024], x_d.ap()[tt * 512 + i * 128: tt * 512 + (i + 1) * 128, :], writes=["xin%d" % i])
                for c in range(8):
                    ps, _, pk = getps()
                    for i in range(4):
                        mk.op("pe", lambda e, c=c, i=i, ps=ps: e.transpose(ps[:, i * 128:(i + 1) * 128], xin[:, i * 1024 + c * 128: i * 1024 + (c + 1) * 128], ident_f[:]),
                              reads=["xin%d" % i, "ident_f"], writes=[pk])
                    if c < 4:
                        pe_ap = V(pet, 4 * NP, c * NP + tt * 8, [[1, 8], [0, 64]])
                    else:
                        pe_ap = V(pet, 4 * NP, (c - 4) * NP + NROW, [[0, 8], [1, 64]])
                    mk.op("dve", lambda e, c=c, pe_ap=pe_ap, ps=ps, xt=xt: e.tensor_tensor(V(xt, 4096, c * 512, [[64, 8], [1, 64]]), V(ps, 512, 0, [[64, 8], [1, 64]]),
                                                                                   pe_ap, ALU.add), reads=[pk, "pet"], writes=[xk])
                    pump(modgen)
                    pump(tabgen, 2)
                mk.dma(V(xs_d, 8 * T, tt * 512, [[T, 8], [1, 512]]), V(xt, 4096, 0, [[512, 8], [1, 512]]), reads=[xk])
                if tt == min(1, NTT - 1):
                    drain(modgen)
                if tt >= 1:
                    norm_tile(XT[(tt - 1) % 2], "xt%d" % ((tt - 1) % 2), XT[(tt - 1) % 2], "xt%d" % ((tt - 1) % 2), gs, modv, ["gs", "modv"], hT_out(tt - 1))
            ckpt("p0load")
            norm_tile(XT[(NTT - 1) % 2], "xt%d" % ((NTT - 1) % 2), XT[(NTT - 1) % 2], "xt%d" % ((NTT - 1) % 2), gs, modv, ["gs", "modv"], hT_out(NTT - 1))
            drain(tabgen)
            ckpt("p0tab")
            mk.barrier()
            ckpt("phase0")

        for l in range(DEPTH):
            w_in_l = w_in_d.ap()[l]
            lay = ExitStack()
            ya = sb("ya", [128, 4 * T], BF16, lay)
            with ExitStack() as ph:
                Bt = sb("Bt", [128, 32 * 2 * 2 * 64], BF16, ph)
                GT = [sb("GT%d" % d, [128, 2 * 4096], BF16, ph) for d in range(2)]
                Tz = sb("Tz", [128, 32 * 128], BF16, ph)
                A1 = sb("A1", [128, 64], F32, ph)
                A2 = sb("A2", [128, 64], F32, ph)
                mk.dma(Bt[:], TB_Bt[l].ap(), writes=["Bt"])
                mk.dma(Tz[:], TB_Tz[l].ap(), writes=["Tz"])
                AAs = sb("AAs", [128, 128], F32, ph)
                mk.dma(AAs[:], TB_A[l].ap(), writes=["AAs"])
                for d in range(2):
                    mk.op("act", lambda e, d=d: e.memzero(GT[d][0:64, 4096:8192]), writes=["GT%d" % d])
                    mk.op("act", lambda e, d=d: e.memzero(GT[d][64:128, 0:4096]), writes=["GT%d" % d])
                    mk.dma(GT[d][0:64, 0:4096], TB_GT[l][d].ap()[0:64, :], writes=["GT%d" % d])
                    mk.dma(GT[d][64:128, 4096:8192], TB_GT[l][d].ap()[64:128, :], writes=["GT%d" % d])
                mk.op("dve", lambda e: e.tensor_copy(A1[:], AAs[:, 0:64]), reads=["AAs"], writes=["A1"])
                mk.op("dve", lambda e: e.tensor_copy(A2[:], AAs[:, 64:128]), reads=["AAs"], writes=["A2"])
                ckpt("tables")
                bufB = sb("bufB", [128, NR * 64], BF16, ph)
                Ut = sb("Ut", [128, 32 * NR], BF16, ph)
                U8 = bufB
                Eb = bufB
                Xb = bufB
                yT = Ut
                for wb in range(2):
                    wv, wk = load_w(w_in_l, 8, wb * 256, 256)
                    for rt in range(RT):
                        for s in range(8):
                            ps, _, pk = getps()
                            for kt in range(8):
                                lhs = V(hT, 8 * T, kt * T + rt * 1024 + s, [[8, 128]])
                                mk.op("pe", lambda e, lhs=lhs, kt=kt, ps=ps: e.matmul(ps[:, 0:256], lhs, wv[:, kt, :], start=(kt == 0), stop=(kt == 7)),
                                      reads=["hT", wk], writes=[pk])
                            oap = V(U8, NR * 64, rt * 4096 + wb * 2048 + s * 16, [[128, 16], [1, 16]])
                            mk.op("act", lambda e, oap=oap, ps=ps: e.copy(oap, V(ps, 512, 0, [[16, 16], [1, 16]])), reads=[pk], writes=["U8"])
                for rt in range(RT):
                    for gb in range(4):
                        ps, psb, pk = getps()
                        for gi in range(8):
                            g = gb * 8 + gi
                            src = U8[:, rt * 4096 + g * 128: rt * 4096 + (g + 1) * 128]
                            mk.op("pe", lambda e, gi=gi, src=src, psb=psb: e.transpose(psb[:, gi * 128:(gi + 1) * 128], src, ident_b[:]),
                                  reads=["U8", "ident_b"], writes=[pk])
                        mk.op("dve", lambda e, gb=gb, rt=rt, psb=psb: e.tensor_copy(V(Ut, 32 * NR, gb * 8 * NR + rt * 128, [[NR, 8], [1, 128]]),
                                                                                   V(psb, 1024, 0, [[128, 8], [1, 128]])), reads=[pk], writes=["Ut%d" % gb])
                dump("Ut", Ut[:], "Ut0")
                ckpt("U")
                for d in range(2):
                    for j in range(16):
                        ps, _, pk = getps()
                        for g2 in range(2):
                            g = 2 * j + g2
                            for comp in range(2):
                                bi = ((g * 2 + d) * 2 + comp) * 64
                                mk.op("pe", lambda e, g2=g2, comp=comp, bi=bi, g=g, ps=ps: e.matmul(ps[g2 * 64:(g2 + 1) * 64, comp * 256: comp * 256 + NR], Bt[:, bi:bi + 64],
                                                                                           Ut[:, g * NR:(g + 1) * NR], start=True, stop=True),
                                      reads=["Bt", "Ut%d" % (g // 8)], writes=[pk])
                        if d == 0:
                            oap = V(Xb, NR * 64, j * NR, [[32 * NR, 2], [1, NR]])
                        else:
                            oap = V(Xb, NR * 64, (16 + j) * NR + NR - 1, [[32 * NR, 2], [-1, NR]])
                        if j % 2 == 0:
                            mk.op("act", lambda e, oap=oap, ps=ps: e.copy(oap, V(ps, 512, 0, [[256, 2], [1, NR]])), reads=[pk], writes=["Xb"])
                        else:
                            mk.op("dve", lambda e, oap=oap, ps=ps: e.tensor_copy(oap, V(ps, 512, 0, [[256, 2], [1, NR]])), reads=[pk], writes=["Xb"])
                mk.barrier()
                dump("Xb", Xb[:], "Xb")
                ckpt("X")
                E2 = sb("E2", [128, 128], F32, ph)
                T1 = sb("T1", [128, 64], F32, ph)
                T2 = sb("T2", [128, 64], F32, ph)
                EX = sb("EX", [128, NSEG * 64], F32, ph)
                OST = sb("OST", [128, NSEG * 64], F32, ph)
                mk.dma(EX[:], s5ext_d.ap()[l], writes=["EX"])
                memset("dve", E2[:], 0.0, "E2_0")
                mk.bufs["E2_1"] = {"w": mk.bufs["E2_0"]["w"], "r": {}}
                RL = NR * 64
                sgtA = [sb("sgtA%d" % i_, [128, 512], BF16, ph) for i_ in range(2)]

                def gate_a_gen():
                    it = 0
                    for wb in range(2):
                        wv, wk = load_w(w_in_l, 8, 512 + wb * 256, 256)
                        for oc2 in range(2):
                            oc = wb * 2 + oc2
                            for tt in range(NTT):
                                ps, _, pk = getps()
                                for kt in range(8):
                                    mk.op("pe", lambda e, ps=ps, oc2=oc2, kt=kt, tt=tt, wv=wv: e.matmul(ps[:], wv[:, kt, oc2 * 128:(oc2 + 1) * 128], hT_ap(kt, tt * 512, 512),
                                                                                               start=(kt == 0), stop=(kt == 7)), reads=[wk, "hT"], writes=[pk])
                                o = oc * T + tt * 512
                                mk.op("act", lambda e, ps=ps, o=o: e.activation(ya[:, o:o + 512], ps[:], AF.Silu), reads=[pk], writes=["ya%d" % oc])
                                yield

                gagen = gate_a_gen()
                for i in range(NR):
                    if i % max(1, NR // 16) == 0:
                        pump(gagen)
                    kb = i // 32
                    co = (i % 2) * 64
                    no = ((i + 1) % 2) * 64
                    ck, nk = "E2_%d" % (i % 2), "E2_%d" % ((i + 1) % 2)
                    cur = E2[:, co:co + 64]
                    ks = "Xs%d" % i
                    if i % 32 == 0:
                        mk.op("dve", lambda e, cur=cur, kb=kb: e.scalar_tensor_tensor(cur, cur, flags[:, 1:2], EX[:, kb * 64:(kb + 1) * 64], ALU.mult, ALU.add),
                              reads=[ck, "flags", "EX"], writes=[ck])
                    mk.op("pool", lambda e, cur=cur: e.tensor_tensor(T1[:], cur, A1[:], ALU.mult), reads=[ck, "A1"], writes=["T1"])
                    mk.op("dve", lambda e, co=co: e.tensor_tensor(V(T2, 64, 0, [[32, 2], [1, 32]]), V(E2, 128, co + 32, [[-32, 2], [1, 32]]), V(A2, 64, 0, [[32, 2], [1, 32]]), ALU.mult),
                          reads=[ck, "A2"], writes=["T2"])
                    mk.op("dve", lambda e, i=i: e.tensor_tensor(T2[:], T2[:], V(Xb, RL, i, [[NR, 64]]), ALU.add), reads=["T2", ks], writes=["T2"])
                    mk.op("act", lambda e, i=i, cur=cur: e.copy(V(Eb, RL, i, [[NR, 64]]), cur), reads=[ck], writes=[ks])
                    mk.op("dve", lambda e, no=no: e.tensor_tensor(E2[:, no:no + 64], T1[:], T2[:], ALU.add), reads=["T1", "T2"], writes=[nk])
                    if (i + 1) % 32 == 0:
                        mk.op("act", lambda e, kb=kb, no=no: e.copy(OST[:, kb * 64:(kb + 1) * 64], E2[:, no:no + 64]), reads=[nk], writes=["OST"])
                drain(gagen)
                mk.dma(s5o_d.ap()[l], OST[:], reads=["OST"])
                mk.barrier()
                Ebr = Bt
                mk.op("act", lambda e: e.copy(V(Ebr, 8192, 0, [[16 * NR, 2], [NR, 16], [1, NR]]), V(Eb, RL, 16 * NR + NR - 1, [[32 * NR, 2], [NR, 16], [-1, NR]])), reads=["Xb"], writes=["Ebr"])
                dump("Eb", Eb[:], "Eb")
                ckpt("chain")
                y8c = sb("y8c", [128, 1024], BF16, ph)
                g1 = sb("g1", [128, 512], F32, ph)
                g2t = sb("g2t", [128, 512], F32, ph)
                UL = 32 * NR
                for fc in range(4):
                    allbanks = []
                    for rt in range(RT):
                        banks = [getps(), getps()]
                        allbanks.append(banks)
                        for gi in range(8):
                            g = fc * 8 + gi
                            j, g2 = g // 2, g % 2
                            ps, _, pk = banks[gi // 4]
                            col0 = (gi % 4) * 128
                            first = True
                            for d in range(2):
                                for comp in range(2):
                                    if d == 0:
                                        lhs = V(Eb, RL, (comp * 32 + j) * NR + rt * 128, [[1, 128]])
                                    else:
                                        lhs = V(Ebr, 8192, (comp * 16 + j) * NR + rt * 128, [[1, 128]])
                                    rhs = GT[d][:, g2 * 4096 + (j * 2 + comp) * 128: g2 * 4096 + (j * 2 + comp + 1) * 128]
                                    mk.op("pe", lambda e, ps=ps, col0=col0, lhs=lhs, rhs=rhs, first=first: e.matmul(ps[:, col0:col0 + 128], lhs, rhs, start=first, stop=False),
                                          reads=["Eb", "Ebr", "GT%d" % d], writes=[pk])
                                    first = False
                            mk.op("pe", lambda e, ps=ps, col0=col0, g=g, rt=rt: e.matmul(ps[:, col0:col0 + 128], Ut[:, g * NR + rt * 128: g * NR + rt * 128 + 128], Tz[:, g * 128:(g + 1) * 128],
                                                                                    start=False, stop=True), reads=["Ut%d" % fc, "Tz"], writes=[pk])
                    for rt in range(RT):
                        banks = allbanks[rt]
                        for bi, (ps, _, pk) in enumerate(banks):
                            gq, gqk = (g1, "g1") if bi == 0 else (g2t, "g2t")
                            mk.op("act", lambda e, ps=ps, gq=gq: e.activation(gq[:], ps[:], AF.Square), reads=[pk], writes=[gqk])
                            mk.op("dve", lambda e, gq=gq: e.tensor_scalar(gq[:], gq[:], 0.044715, 1.0, ALU.mult, ALU.add), reads=[gqk], writes=[gqk])
                            mk.op("dve", lambda e, ps=ps, gq=gq: e.tensor_tensor(gq[:], gq[:], ps[:], ALU.mult), reads=[gqk, pk], writes=[gqk])
                            mk.op("act", lambda e, gq=gq: e.activation(gq[:], gq[:], AF.Sigmoid, scale=1.5957691216), reads=[gqk], writes=[gqk])
                            oap = V(y8c, 1024, bi * 64, [[16, 4], [128, 8], [1, 16]])
                            mk.op("dve", lambda e, ps=ps, oap=oap, gq=gq: e.tensor_tensor(oap, V(gq, 512, 0, [[128, 4], [16, 8], [1, 16]]), V(ps, 512, 0, [[128, 4], [16, 8], [1, 16]]), ALU.mult),
                                  reads=[gqk, pk], writes=["y8c"])
                        ps, psb, pk = getps()
                        for s_ in range(8):
                            mk.op("pe", lambda e, s_=s_, psb=psb: e.transpose(psb[:, s_ * 128:(s_ + 1) * 128], y8c[:, s_ * 128:(s_ + 1) * 128], ident_b[:]), reads=["y8c", "ident_b"], writes=[pk])
                        mk.op("act", lambda e, fc=fc, rt=rt, psb=psb: e.copy(V(yT, UL, fc * T + rt * 1024, [[1, 8], [8, 128]]), V(psb, 1024, 0, [[128, 8], [1, 128]])),
                              reads=[pk], writes=["Ut%d" % fc])
                dump("yT", yT[:, 0:4 * T], "Ut0")
                ckpt("y")
                bglu = sb("bglu", [128, 4], F32, ph)
                load_vec(bglu[:], "bglu", b_glu_d.ap()[l].rearrange("(c p) -> c p", p=128), 4)
                wv, wk = load_w(w_glu_d.ap()[l], 4, 0, 512)
                for oc in range(4):
                    for tt in range(NTT):
                        ps, _, pk = getps()
                        for kt in range(4):
                            mk.op("pe", lambda e, ps=ps, oc=oc, kt=kt, tt=tt: e.matmul(ps[:], wv[:, kt, oc * 128:(oc + 1) * 128], yT[:, kt * T + tt * 512: kt * T + (tt + 1) * 512],
                                                                                  start=(kt == 0), stop=(kt == 3)), reads=[wk, "Ut0", "Ut1", "Ut2", "Ut3"], writes=[pk])
                        sg_, sgk = sgtA[(oc * NTT + tt) % 2], "sgtA%d" % ((oc * NTT + tt) % 2)
                        mk.op("act", lambda e, ps=ps, oc=oc, sg_=sg_: e.activation(sg_[:], ps[:], AF.Sigmoid, bias=bglu[:, oc:oc + 1], scale=1.0), reads=[pk, "bglu"], writes=[sgk])
                        o = oc * T + tt * 512
                        mk.op("dve", lambda e, o=o, sg_=sg_: e.tensor_tensor(sg_[:], yT[:, o:o + 512], sg_[:], ALU.mult), reads=["Ut%d" % oc, sgk], writes=[sgk])
                        mk.op("dve", lambda e, o=o, sg_=sg_: e.tensor_tensor(ya[:, o:o + 512], ya[:, o:o + 512], sg_[:], ALU.mult), reads=["ya%d" % oc, sgk], writes=["ya%d" % oc])
                dump("ya", ya[:], "ya0")
                mk.barrier()
                ckpt("s5")

            yb = sb("yb", [128, 4 * T], BF16, lay)
            with ExitStack() as ph:
                GL = sb("GL", [128, T], BF16, ph)
                wgu = sb("wgu", [16, 512], BF16, ph)
                nbg = sb("nbg", [128, 4], F32, ph)
                gng = sb("gng", [128, 4], F32, ph)
                QF = sb("QF", [128, T], BF16, ph)
                KF = sb("KF", [128, T], BF16, ph)
                VT = sb("VT", [128, NT * 256], BF16, ph)
                LSPd = [sb("LSP%d" % i_, [128, T], F32, ph) for i_ in range(2)]
                CSd = [sb("CS%d" % i_, [128, T], F32, ph) for i_ in range(2)]
                TOTd = [sb("TOT%d" % i_, [128, NT], F32, ph) for i_ in range(2)]
                EBL = [sb("EBL%d" % d, [128, NT], F32, ph) for d in range(2)]
                QE = [sb("QE%d" % d, [128, 2 * T], BF16, ph) for d in range(2)]
                KE = [sb("KE%d" % d, [128, T], BF16, ph) for d in range(2)]
                KET = [sb("KET%d" % d, [128, NT * 128], BF16, ph) for d in range(2)]
                SH = [sb("SH%d" % d, [128, NT * 128], BF16, ph) for d in range(2)]
                PM = [sb("Pm%d" % i, [128, 512], BF16, ph) for i in range(4)]
                TS = [sb("Tst%d" % i, [128, 128], F32, ph) for i in range(4)]
                SX = [sb("Sx%d" % i, [128, 128], F32, ph) for i in range(4)]
                GEXT = [sb("gext%d" % i, [128, 128], F32, ph) for i in range(2)]
                rr = {"pm": 0, "sx": 0, "on": 0}
                OSQ = [sb("osq%d" % i, [128, 512], BF16, ph) for i in range(1)]
                ORS = [sb("ors%d" % i, [128, 512], F32, ph) for i in range(1)]
                with ncdma():
                    mk.dma(ORS[0][0:16, :].rearrange("r (d n) -> r d n", d=2), wgu_d.ap()[l].rearrange("d r n -> r d n"), writes=["ors0"])
                    pass
                load_vec(nbg[:], "nbg", bg_d.ap()[l].rearrange("d (c p) -> (d c) p", p=128), 4)
                load_vec(gng[:], "gng", gng_d.ap()[l].rearrange("(c p) -> c p", p=128), 4)
                mk.op("pool", lambda e: e.tensor_copy(wgu[:], ORS[0][0:16, :]), reads=["ors0"], writes=["wgu"])
                mk.op("dve", lambda e: e.tensor_scalar(nbg[:], nbg[:], -1.0, None, ALU.mult), reads=["nbg"], writes=["nbg"])
                wv, wk = load_w(w_in_l, 8, 2560, 16)
                for tt in range(NTT):
                    ps, _, pk = getps()
                    for kt in range(8):
                        mk.op("pe", lambda e, ps=ps, kt=kt, tt=tt: e.matmul(ps[0:16, :], wv[:, kt, :], hT_ap(kt, tt * 512, 512), start=(kt == 0), stop=(kt == 7)),
                              reads=[wk, "hT"], writes=[pk])
                    mk.op("act", lambda e, ps=ps, tt=tt: e.copy(GL[0:16, tt * 512:(tt + 1) * 512], ps[0:16, :]), reads=[pk], writes=["GL"])
                ckpt("g_gl")
                for c in range(2):
                    for which, dst, col in (("q", QF, 1024 + c * 128), ("k", KF, 1280 + c * 128)):
                        wv, wk = load_w(w_in_l, 8, col, 128)
                        for tt in range(NTT):
                            ps, _, pk = getps()
                            for kt in range(8):
                                mk.op("pe", lambda e, ps=ps, kt=kt, tt=tt, wv=wv: e.matmul(ps[:], wv[:, kt, :], hT_ap(kt, tt * 512, 512), start=(kt == 0), stop=(kt == 7)),
                                      reads=[wk, "hT"], writes=[pk])
                            if which == "q":
                                mk.op("act", lambda e, ps=ps, tt=tt: e.mul(QF[:, tt * 512:(tt + 1) * 512], ps[:], 0.125), reads=[pk], writes=["QF"])
                            else:
                                mk.op("act", lambda e, ps=ps, tt=tt: e.copy(KF[:, tt * 512:(tt + 1) * 512], ps[:]), reads=[pk], writes=["KF"])
                    wv, wk = load_w(w_in_l, 8, 1536 + c * 256, 256)
                    for n in range(NT):
                        ps, _, pk = getps()
                        for kt in range(8):
                            mk.op("pe", lambda e, ps=ps, kt=kt, n=n, wv=wv: e.matmul(ps[:, 0:256], hT_ap(kt, n * 128, 128), wv[:, kt, :], start=(kt == 0), stop=(kt == 7)),
                                  reads=[wk, "hT"], writes=[pk])
                        mk.op("act", lambda e, ps=ps, n=n: e.copy(VT[:, n * 256:(n + 1) * 256], ps[:, 0:256]), reads=[pk], writes=["VT"])
                    ckpt("g_qkv%d" % c)
                    def prep_gen(d):
                        LSP, CS, TOT = LSPd[d], CSd[d], TOTd[d]
                        kl, kc, kt_ = "LSP%d" % d, "CS%d" % d, "TOT%d" % d
                        for tt in range(NTT):
                            ps, _, pk = getps()
                            mk.op("pe", lambda e, ps=ps, tt=tt: e.matmul(ps[:], wgu[0:16, d * 256 + c * 128: d * 256 + (c + 1) * 128], GL[0:16, tt * 512:(tt + 1) * 512], start=True, stop=True),
                                  reads=["wgu", "GL"], writes=[pk])
                            mk.op("act", lambda e, ps=ps, tt=tt: e.activation(LSP[:, tt * 512:(tt + 1) * 512], ps[:], AF.Exp, bias=nbg[:, d * 2 + c: d * 2 + c + 1], scale=-1.0),
                                  reads=[pk, "nbg"], writes=[kl])
                            yield
                        mk.op("act", lambda e: e.activation(LSP[:], LSP[:], AF.Ln, bias=ones_f[:, 0:1], scale=1.0), reads=[kl, "ones_f"], writes=[kl])
                        mk.op("act", lambda e: e.memzero(QE[d][64:128, 0:T]), writes=["QE%d" % d])
                        mk.op("act", lambda e: e.memzero(QE[d][0:64, T:2 * T]), writes=["QE%d" % d])
                        yield
                        mk.op("dve", lambda e: e.tensor_tensor_scan(CS[:], maskc[:], LSP[:], 0.0, ALU.mult, ALU.add), reads=["maskc", kl], writes=[kc])
                        yield
                        mk.op("dve", lambda e: e.tensor_copy(TOT[:], V(CS, T, 127, [[128, NT]])), reads=[kc], writes=[kt_])
                        if d == 1:
                            mk.op("dve", lambda e: e.tensor_tensor(V(CS, T, 0, [[128, NT], [1, 128]]), V(TOT, NT, 0, [[1, NT], [0, 128]]), V(CS, T, 0, [[128, NT], [1, 128]]), ALU.subtract),
                                  reads=[kt_, kc], writes=[kc])
                            yield
                            mk.op("dve", lambda e: e.tensor_tensor(CS[:], CS[:], LSP[:], ALU.add), reads=[kc, kl], writes=[kc])
                        mk.op("act", lambda e: e.activation(EBL[d][:], TOT[:], AF.Exp, scale=-1.0 / 16.0), reads=[kt_], writes=["EBL%d" % d])
                        yield
                        mk.op("act", lambda e: e.activation(LSP[:], CS[:], AF.Exp, scale=-1.0 / 16.0), reads=[kc], writes=[kl])
                        yield
                        mk.op("dve", lambda e: e.tensor_tensor(QE[d][0:64, 0:T], QF[0:64, :], LSP[0:64, :], ALU.mult), reads=["QF", kl], writes=["QE%d" % d])
                        mk.op("dve", lambda e: e.tensor_tensor(QE[d][64:128, T:2 * T], QF[64:128, :], LSP[64:128, :], ALU.mult), reads=["QF", kl], writes=["QE%d" % d])
                        yield
                        mk.op("act", lambda e: e.activation(LSP[:], CS[:], AF.Exp, scale=1.0 / 16.0), reads=[kc], writes=[kl])
                        yield
                        mk.op("dve", lambda e: e.tensor_tensor(KE[d][:], KF[:], LSP[:], ALU.mult), reads=["KF", kl], writes=["KE%d" % d])
                        yield
                        for nb in range(NT // 8):
                            ps, psb, pk = getps()
                            for ni in range(8):
                                n = nb * 8 + ni
                                mk.op("pe", lambda e, psb=psb, ni=ni, n=n: e.transpose(psb[:, ni * 128:(ni + 1) * 128], KE[d][:, n * 128:(n + 1) * 128], ident_b[:]),
                                      reads=["KE%d" % d, "ident_b"], writes=[pk])
                            mk.op("dve", lambda e, psb=psb, nb=nb: e.tensor_copy(KET[d][:, nb * 1024:(nb + 1) * 1024], psb[:, 0:1024]), reads=[pk], writes=["KET%d" % d])
                            yield

                    pg = [prep_gen(0), prep_gen(1)]
                    alive = [True, True]
                    while any(alive):
                        for d in range(2):
                            if alive[d]:
                                alive[d] = pump(pg[d])
                    ckpt("g_prep%d" % c)
                    for d in range(2):
                        mk.dma(GEXT[d][:], glaext_d.ap()[l, d, c], writes=["gext%d" % d])
                    pe_prev = [None, None]
                    for k in range(NT):
                        for d in range(2):
                            gext, gk = GEXT[d], "gext%d" % d
                            pe_ = pe_prev[d]
                            n = k if d == 0 else NT - 1 - k
                            ps, _, pk = getps()
                            for h2 in range(2):
                                mk.op("pe", lambda e, ps=ps, h2=h2, n=n, d=d: e.matmul(ps[h2 * 64:(h2 + 1) * 64, 0:128], KET[d][:, n * 128 + h2 * 64: n * 128 + (h2 + 1) * 64],
                                                                                  VT[:, n * 256 + h2 * 128: n * 256 + (h2 + 1) * 128], start=True, stop=True),
                                      reads=["KET%d" % d, "VT"], writes=[pk])
                            Tc, Tn = TS[d * 2 + k % 2], TS[d * 2 + (k + 1) % 2]
                            tck, tnk = "Tst%d" % (d * 2 + k % 2), "Tst%d" % (d * 2 + (k + 1) % 2)
                            if k == 0:
                                mk.op("act", lambda e, n=n, d=d, gext=gext: e.copy(SH[d][:, n * 128:(n + 1) * 128], gext[:]), reads=[gk], writes=["SH%d" % d])
                                mk.op("dve", lambda e, ps=ps, Tn=Tn, gext=gext: e.tensor_tensor(Tn[:], ps[:, 0:128], gext[:], ALU.add), reads=[pk, gk], writes=[tnk])
                            elif k % 2 == 0:
                                si = rr["sx"]; rr["sx"] = (si + 1) % 4
                                Sx, sk = SX[si], "Sx%d" % si
                                mk.op("dve", lambda e, pe_=pe_, Sx=Sx, Tc=Tc: e.tensor_scalar(Sx[:], Tc[:], pe_, flags[:, 1:2], ALU.mult, ALU.mult), reads=[tck, "EBL%d" % d, "flags"], writes=[sk])
                                mk.op("act", lambda e, n=n, d=d, Sx=Sx: e.copy(SH[d][:, n * 128:(n + 1) * 128], Sx[:]), reads=[sk], writes=["SH%d" % d])
                                mk.op("dve", lambda e, ps=ps, Sx=Sx, Tn=Tn: e.tensor_tensor(Tn[:], ps[:, 0:128], Sx[:], ALU.add), reads=[pk, sk], writes=[tnk])
                            else:
                                mk.op("act", lambda e, n=n, d=d, pe_=pe_, Tc=Tc: e.activation(SH[d][:, n * 128:(n + 1) * 128], Tc[:], AF.Copy, scale=pe_), reads=[tck, "EBL%d" % d], writes=["SH%d" % d])
                                mk.op("dve", lambda e, ps=ps, pe_=pe_, Tc=Tc, Tn=Tn: e.scalar_tensor_tensor(Tn[:], Tc[:], pe_, ps[:, 0:128], ALU.mult, ALU.add), reads=[tck, "EBL%d" % d, pk], writes=[tnk])
                            pe_ = EBL[d][:, n:n + 1]
                            pe_prev[d] = pe_
                            if k % 2 == 1:
                                si = rr["sx"]; rr["sx"] = (si + 1) % 4
                                Sx, sk = SX[si], "Sx%d" % si
                                mk.op("pool", lambda e, pe_=pe_, Sx=Sx, Tn=Tn: e.tensor_scalar(Sx[:], Tn[:], pe_, None, ALU.mult), reads=[tnk, "EBL%d" % d], writes=[sk])
                                mk.dma(glao_d.ap()[l, k // 2, d, c], Sx[:], reads=[sk])
                    ckpt("g_chain%d" % c)
                    items = [(h2, tt) for h2 in range(2) for tt in range(NTT)]

                    def emit_scores(h2, tt):
                        banks = []
                        for half in range(2):
                            ps, _, pk = getps()
                            for nl in range(2):
                                n = tt * 4 + half * 2 + nl
                                for d in range(2):
                                    col = (nl * 2 + d) * 128
                                    mk.op("pe", lambda e, ps=ps, d=d, n=n, h2=h2, col=col: e.matmul(ps[:, col:col + 128], KE[d][:, n * 128:(n + 1) * 128],
                                                                                             QE[d][:, h2 * T + n * 128: h2 * T + (n + 1) * 128], start=True, stop=True),
                                          reads=["KE%d" % d, "QE%d" % d], writes=[pk])
                            pi = rr["pm"]; rr["pm"] = (pi + 1) % 4
                            Pm, pmk = PM[pi], "Pm%d" % pi
                            mk.op("dve", lambda e, ps=ps, Pm=Pm: e.tensor_tensor(Pm[:], ps[:], mask4[:], ALU.mult), reads=[pk, "mask4"], writes=[pmk])
                            banks.append((Pm, pmk))
                        return banks

                    pend = emit_scores(*items[0])
                    for idx, (h2, tt) in enumerate(items):
                        h = 2 * c + h2
                        cur_b = pend
                        if idx + 1 < len(items):
                            pend = emit_scores(*items[idx + 1])
                        pso, _, pko = getps()
                        for ni in range(4):
                            n = tt * 4 + ni
                            Pm, pmk = cur_b[ni // 2]
                            for d in range(2):
                                pc = ((ni % 2) * 2 + d) * 128
                                mk.op("pe", lambda e, pso=pso, d=d, n=n, h2=h2, ni=ni, Pm=Pm, pc=pc: e.matmul(pso[:, ni * 128:(ni + 1) * 128], VT[:, n * 256 + h2 * 128: n * 256 + (h2 + 1) * 128],
                                                                                                   Pm[:, pc:pc + 128], start=(d == 0), stop=False), reads=["VT", pmk], writes=[pko])
                                mk.op("pe", lambda e, pso=pso, d=d, n=n, h2=h2, ni=ni: e.matmul(pso[:, ni * 128:(ni + 1) * 128], SH[d][:, n * 128:(n + 1) * 128],
                                                                                        QE[d][:, h2 * T + n * 128: h2 * T + (n + 1) * 128], start=False, stop=(d == 1)),
                                      reads=["SH%d" % d, "QE%d" % d], writes=[pko])
                        oi = 0
                        osq, ors, osk, ork = OSQ[oi], ORS[oi], "osq%d" % oi, "ors%d" % oi
                        mk.op("act", lambda e, pso=pso, osq=osq: e.activation(osq[:], pso[:], AF.Square), reads=[pko], writes=[osk])
                        ps2, _, pk2 = getps()
                        mk.op("pe", lambda e, ps2=ps2, osq=osq: e.matmul(ps2[:], onesH[:], osq[:], start=True, stop=True), reads=["onesH", osk], writes=[pk2])
                        mk.op("act", lambda e, ps2=ps2, ors=ors: e.activation(ors[:], ps2[:], AF.Sqrt, bias=eps_t[:, 0:1], scale=1.0), reads=[pk2, "eps_t"], writes=[ork])
                        mk.op("dve", lambda e, ors=ors: e.reciprocal(ors[:], ors[:]), reads=[ork], writes=[ork])
                        o = h * T + tt * 512
                        mk.op("dve", lambda e, pso=pso, o=o, h=h, ors=ors: e.scalar_tensor_tensor(yb[:, o:o + 512], pso[:], gng[:, h:h + 1], ors[:], ALU.mult, ALU.mult),
                              reads=[pko, "gng", ork], writes=["yb"])
                ckpt("g_out")
                dump("yb0", yb[:], "yb")
                sgt = OSQ[0]
                for wb in range(2):
                    wv, wk = load_w(w_in_l, 8, 2048 + wb * 256, 256)
                    for oc2 in range(2):
                        oc = wb * 2 + oc2
                        for tt in range(NTT):
                            ps, _, pk = getps()
                            for kt in range(8):
                                mk.op("pe", lambda e, ps=ps, oc2=oc2, kt=kt, tt=tt, wv=wv: e.matmul(ps[:], wv[:, kt, oc2 * 128:(oc2 + 1) * 128], hT_ap(kt, tt * 512, 512),
                                                                                           start=(kt == 0), stop=(kt == 7)), reads=[wk, "hT"], writes=[pk])
                            mk.op("act", lambda e, ps=ps: e.activation(sgt[:], ps[:], AF.Silu), reads=[pk], writes=["osq0"])
                            o = oc * T + tt * 512
                            mk.op("dve", lambda e, o=o: e.tensor_tensor(yb[:, o:o + 512], yb[:, o:o + 512], sgt[:], ALU.mult), reads=["yb", "osq0"], writes=["yb"])
                dump("yb", yb[:], "yb")
                mk.barrier()
                ckpt("gla")

            with ExitStack() as ph:
                mg = sb("mg", [128, 8 * T], BF16, ph)
                gate_l = sb("gate_l", [128, 8], F32, ph)
                mk.op("dve", lambda e: e.tensor_copy(gate_l[:], modv[:, 16:24]), reads=["modv"], writes=["gate_l"])
                last = (l == DEPTH - 1)
                modgen = compute_mod_gen(l + 1) if not last else iter(())
                tabgen = tables_gen(l + 1) if not last else iter(())
                sg = sb("sg", [128, 512], F32, ph)
                tAb = sb("tAb", [128, 512], F32, ph)
                for part, (wp_d, ysrc, yk, mcol) in enumerate(((w_pa_d, ya, "ya", 2576), (w_pb_d, yb, "yb", 3600))):
                    for blk in range(4):
                        wp, wpk = load_w(wp_d.ap()[l], 4, blk * 256, 256)
                        wm, wmk = load_w(w_in_l, 8, mcol + blk * 256, 256)
                        for oc2 in range(2):
                            oc = blk * 2 + oc2
                            for tt in range(NTT):
                                ps1, _, pk1 = getps()
                                ps2, _, pk2 = getps()
                                for kt in range(8):
                                    mk.op("pe", lambda e, ps1=ps1, kt=kt, tt=tt, oc2=oc2, wm=wm: e.matmul(ps1[:], wm[:, kt, oc2 * 128:(oc2 + 1) * 128], hT_ap(kt, tt * 512, 512), start=(kt == 0), stop=(kt == 7)),
                                          reads=[wmk, "hT"], writes=[pk1])
                                for kt in range(4):
                                    mk.op("pe", lambda e, ps2=ps2, kt=kt, tt=tt, oc2=oc2, wp=wp, ysrc=ysrc: e.matmul(ps2[:], wp[:, kt, oc2 * 128:(oc2 + 1) * 128], ysrc[:, kt * T + tt * 512: kt * T + (tt + 1) * 512],
                                                                                                        start=(kt == 0), stop=(kt == 3)), reads=[wpk, yk], writes=[pk2])
                                mk.op("act", lambda e, ps1=ps1: e.activation(sg[:], ps1[:], AF.Sigmoid), reads=[pk1], writes=["sg"])
                                o = oc * T + tt * 512
                                if part == 0:
                                    mk.op("dve", lambda e, ps2=ps2, o=o: e.tensor_tensor(mg[:, o:o + 512], sg[:], ps2[:], ALU.mult), reads=["sg", pk2], writes=["mg"])
                                else:
                                    mk.op("dve", lambda e, ps2=ps2: e.tensor_tensor(tAb[:], sg[:], ps2[:], ALU.mult), reads=["sg", pk2], writes=["tAb"])
                                    mk.op("dve", lambda e, o=o: e.tensor_tensor(mg[:, o:o + 512], mg[:, o:o + 512], tAb[:], ALU.add), reads=["mg", "tAb"], writes=["mg"])
                                pump(tabgen, 1)
                            if tt == NTT - 1:
                                pump(modgen)
                dump("mg", mg[:], "mg")
                drain(tabgen)
                mk.barrier()
                ckpt("mg")
                ckpt("m_merge")
                wo = sb("wo", [128, 8 * 1024], BF16, ph)
                for blk in range(4):
                    i = wrr[0]; wrr[0] = (i + 1) % 2
                    stv = WST[i][:, 0:2048].rearrange("p (k n) -> p k n", k=8)
                    src = w_o_d.ap()[l][:, blk * 256:(blk + 1) * 256].rearrange("(k p) n -> p k n", p=128)
                    with ncdma():
                        mk.dma(stv, src, writes=["wst%d" % i])
                    mk.op("act", lambda e, blk=blk, stv=stv: e.copy(V(wo, 8192, blk * 256, [[1024, 8], [1, 256]]), stv), reads=["wst%d" % i], writes=["wo"])
                drain(modgen)
                ckpt("m_wo_mod")
                XO = [sb("xo%d" % i, [128, 8 * 512], F32, ph) for i in range(2)]
                NB["sq"] = sb("sq1", [128, 8 * 512], BF16, ph)
                NB["rstd_b"] = sb("rstd1", [128, 512], F32, ph)
                NB["rs_tmp"] = sb("rstmp1", [128, 512], F32, ph)
                if last:
                    yout = sb("yout", [128, 2 * 1024], F32, ph)
                for tt in range(NTT):
                    xo, xok = XO[tt % 2], "xo%d" % (tt % 2)
                    tmpn = xo
                    mk.dma(V(xo, 4096, 0, [[512, 8], [1, 512]]), V(xs_d, 8 * T, tt * 512, [[T, 8], [1, 512]]), writes=[xok])
                    for oc in range(8):
                        ps, _, pk = getps()
                        for kt in range(8):
                            mk.op("pe", lambda e, ps=ps, kt=kt, oc=oc, tt=tt: e.matmul(ps[:], wo[:, kt * 1024 + oc * 128: kt * 1024 + (oc + 1) * 128], mg[:, kt * T + tt * 512: kt * T + (tt + 1) * 512],
                                                                                  start=(kt == 0), stop=(kt == 7)), reads=["wo", "mg"], writes=[pk])
                        mk.op("dve", lambda e, ps=ps, oc=oc, xo=xo: e.scalar_tensor_tensor(xo[:, oc * 512:(oc + 1) * 512], ps[:], gate_l[:, oc:oc + 1], xo[:, oc * 512:(oc + 1) * 512], ALU.mult, ALU.add),
                              reads=[pk, "gate_l", xok], writes=[xok])
                    if not last:
                        mk.dma(V(xs_d, 8 * T, tt * 512, [[T, 8], [1, 512]]), V(xo, 4096, 0, [[512, 8], [1, 512]]), reads=[xok])
                        norm_tile(xo, xok, tmpn, xok, gs, modv, ["gs", "modv"], hT_out(tt))
                    else:
                        def fin(c, src, sc_, sh_, keys, xok=xok):
                            mk.op("act", lambda e: e.activation(src, src, AF.Copy, scale=sc_), reads=keys, writes=[xok])
                        norm_tile(xo, xok, tmpn, xok, fng, zero8, ["fng"], fin)
                        for i in range(4):
                            for cb in range(2):
                                ps, _, pk = getps()
                                for cc in range(4):
                                    c = cb * 4 + cc
                                    mk.op("pe", lambda e, ps=ps, cc=cc, c=c, i=i, tmpn=tmpn: e.transpose(ps[:, cc * 128:(cc + 1) * 128], tmpn[:, c * 512 + i * 128: c * 512 + (i + 1) * 128], ident_f[:]),
                                          reads=[xok, "ident_f"], writes=[pk])
                                mk.op("act", lambda e, ps=ps, i=i, cb=cb: e.copy(yout[:, (i % 2) * 1024 + cb * 512: (i % 2) * 1024 + (cb + 1) * 512], ps[:]), reads=[pk], writes=["yout%d" % (i % 2)])
                            mk.dma(y_d.ap()[tt * 512 + i * 128: tt * 512 + (i + 1) * 128, :], yout[:, (i % 2) * 1024:(i % 2 + 1) * 1024], reads=["yout%d" % (i % 2)])
                mk.barrier()
            lay.close()


        stopped = False
    except _Stop:
        stopped = True
    mk.finish()
    if not stopped:
        es.close()
    return nc, mk


NSEG_FULL = 8


def make_core_inputs(inputs, core, NSEG=NSEG_FULL):
    f = np.float32
    T = NSEG * 256
    m = {}
    is_sample = core < 4
    if is_sample:
        b = core
        m["x"] = np.ascontiguousarray(inputs["x_sample"][b], dtype=f)
        cond = inputs["c"][b]
    else:
        p0 = (core - 4) * 4
        xp = np.zeros((T, D), f)
        xp[: 4 * 256] = inputs["x_prompt"][p0:p0 + 4].reshape(4 * 256, D)
        m["x"] = xp
        cond = inputs["c_ctx"]
    m["cond"] = np.ascontiguousarray(np.asarray(cond, f).reshape(8, 128).T)
    fl = np.zeros((128, 2), f)
    if is_sample:
        fl[:, 0] = 1.0
        fl[:, 1] = 1.0
    m["flags"] = fl
    s5ext = np.zeros((DEPTH, 128, NSEG, 2, 2, 16), f)
    glaext = np.zeros((DEPTH, 2, 2, 128, 128), f)
    if is_sample:
        for comp, key in enumerate(("state_s5_re", "state_s5_im")):
            st = np.asarray(inputs[key][b], f)
            st = st.reshape(DEPTH, 2, 16, 2, 64)
            s5ext[:, :, 0, comp, :, :] = st.transpose(0, 3, 4, 1, 2).reshape(DEPTH, 128, 2, 16)
        sg = np.asarray(inputs["state_gla"][b], f)
        glaext[:] = sg.reshape(DEPTH, 2, 2, 128, 128)
    m["s5ext"] = np.ascontiguousarray(s5ext.reshape(DEPTH, 128, NSEG * 64))
    m["glaext"] = glaext
    for k in ("norm_g", "w_mod", "b_mod", "w_in", "gla_wg_up", "gla_bg", "gla_norm_g", "s5_lam_re", "s5_lam_im", "s5_log_dt",
              "s5_b_re", "s5_b_im", "s5_c_re", "s5_c_im", "s5_d", "w_glu", "b_glu", "w_pa", "w_pb", "w_o", "final_norm_g"):
        m[k] = np.ascontiguousarray(inputs[k], dtype=f)
    return m


_CACHE = {}


def kernel(**inputs):
    inputs = {k: np.asarray(v) for k, v in inputs.items()}
    if "nc" not in _CACHE:
        plan = build(NSEG_FULL)[1]
        _CACHE["nc"] = build(NSEG_FULL, needed=set(plan.used))[0]
    nc = _CACHE["nc"]
    in_maps = [make_core_inputs(inputs, c) for c in range(8)]
    res = run_bass_kernel_spmd(nc, in_maps, core_ids=list(range(8)))
    r = res.results
    f = np.float32
    y_prompt = np.zeros((16, 256, D), f)
    y_sample = np.zeros((4, 2048, D), f)
    new_re = np.zeros((16, DEPTH, 2, 32, 64), f)
    new_im = np.zeros((16, DEPTH, 2, 32, 64), f)
    new_gla = np.zeros((16, DEPTH, 2, 4, 64, 128), f)
    for c in range(4):
        y_sample[c] = r[c]["y"]
    for c in range(4, 8):
        p0 = (c - 4) * 4
        y_prompt[p0:p0 + 4] = r[c]["y"][: 4 * 256].reshape(4, 256, D)
        s5o = r[c]["s5o"].reshape(DEPTH, 2, 64, NSEG_FULL, 2, 2, 16)
        glao = r[c]["glao"].reshape(DEPTH, NSEG_FULL, 2, 4, 64, 128)
        for q in range(4):
            for d in range(2):
                kb = q if d == 0 else NSEG_FULL - 1 - q
                blk = s5o[:, :, :, kb, :, d, :]
                arr = blk.transpose(0, 3, 4, 1, 2).reshape(DEPTH, 2, 32, 64)
                new_re[p0 + q, :, d] = arr[:, 0]
                new_im[p0 + q, :, d] = arr[:, 1]
                new_gla[p0 + q, :, d] = glao[:, kb, d]
    return (y_prompt, y_sample, new_re, new_im, new_gla)
```
